# Optimizing a Trainium2 kernel written in Bass

```python
import jax, jax.numpy as jnp
from jax import lax
import numpy as np

D_MODEL = 2048
BATCH = 4
SEQ = 2048
DEPTH = 1

N_HEADS = 16
QK_NOPE_DIM = 128
QK_ROPE_DIM = 64
V_HEAD_DIM = 128
Q_LORA_RANK = 512
KV_LORA_RANK = 512
ROPE_THETA = 10000.0
Q_BLOCK = 128

SGU_GROUPS = 8
SGU_GROUP_DIM = 128
SGU_WIDTH = SGU_GROUPS * SGU_GROUP_DIM
CHUNK = 128

D_FF = -(-8 * D_MODEL // (3 * 256)) * 256

N_BRANCH = 2
RMS_EPS = 1e-6

D_IN = Q_LORA_RANK + KV_LORA_RANK + QK_ROPE_DIM + 2 * SGU_WIDTH + N_BRANCH * D_MODEL

kernel_name = "hybrid_mla_sgu_gated_block"


def rms_norm(x, g):
    xf = x.astype(jnp.float32)
    y = xf * lax.rsqrt(jnp.mean(xf * xf, axis=-1, keepdims=True) + RMS_EPS)
    return (y * g.astype(jnp.float32)).astype(x.dtype)


def rope_tables(positions):
    inv_freq = ROPE_THETA ** (-jnp.arange(0, QK_ROPE_DIM, 2, dtype=jnp.float32) / QK_ROPE_DIM)
    ang = positions.astype(jnp.float32)[..., None] * inv_freq
    return jnp.cos(ang), jnp.sin(ang)


def apply_rope(x, cos, sin):
    xf = x.astype(jnp.float32)
    x1, x2 = jnp.split(xf, 2, axis=-1)
    return jnp.concatenate([x1 * cos - x2 * sin, x2 * cos + x1 * sin], axis=-1).astype(x.dtype)


def mla(q_lat, kv_lat, k_pe, cos, sin, q_norm_g, w_uq, kv_norm_g, w_ukv):
    B, S, _ = q_lat.shape
    q = (rms_norm(q_lat, q_norm_g) @ w_uq).reshape(B, S, N_HEADS, QK_NOPE_DIM + QK_ROPE_DIM)
    q_nope = q[..., :QK_NOPE_DIM]
    q_pe = apply_rope(q[..., QK_NOPE_DIM:], cos[:, :, None, :], sin[:, :, None, :])
    kv = (rms_norm(kv_lat, kv_norm_g) @ w_ukv).reshape(B, S, N_HEADS, QK_NOPE_DIM + V_HEAD_DIM)
    k_nope = kv[..., :QK_NOPE_DIM]
    v = kv[..., QK_NOPE_DIM:]
    k_pe = apply_rope(k_pe, cos, sin)
    scale = (QK_NOPE_DIM + QK_ROPE_DIM) ** -0.5
    outs = []
    for i in range(S // Q_BLOCK):
        q0 = i * Q_BLOCK
        k_end = q0 + Q_BLOCK
        s = (jnp.einsum('bqhd,bkhd->bhqk', q_nope[:, q0:k_end], k_nope[:, :k_end])
             + jnp.einsum('bqhd,bkd->bhqk', q_pe[:, q0:k_end], k_pe[:, :k_end]))
        s = s.astype(jnp.float32) * scale
        causal = (q0 + jnp.arange(Q_BLOCK))[:, None] >= jnp.arange(k_end)[None, :]
        s = jnp.where(causal, s, jnp.finfo(jnp.float32).min)
        p = jax.nn.softmax(s, axis=-1).astype(v.dtype)
        outs.append(jnp.einsum('bhqk,bkhd->bqhd', p, v[:, :k_end]))
    return jnp.concatenate(outs, axis=1).reshape(B, S, N_HEADS * V_HEAD_DIM)


def sgu(uv, norm_g, w_s, b_s):
    B, S, _ = uv.shape
    uv = jax.nn.gelu(uv)
    u, v = uv[..., :SGU_WIDTH], uv[..., SGU_WIDTH:]
    v = rms_norm(v, norm_g).reshape(B, S // CHUNK, CHUNK, SGU_GROUPS, SGU_GROUP_DIM)
    tril = jnp.tril(jnp.ones((CHUNK, CHUNK), dtype=bool))
    ws = jnp.where(tril[None], w_s, jnp.zeros_like(w_s))
    mixed = jnp.einsum('gts,bnsgd->bntgd', ws, v) + b_s.T[None, None, :, :, None]
    return u * mixed.reshape(B, S, SGU_WIDTH)


def setup_inputs(seed: int = 0) -> dict:
    key = jax.random.key(seed)
    ks = jax.random.split(key, 24)
    f32 = jnp.float32

    def w(k, shape, fan_in):
        return jax.random.normal(k, shape, f32) * fan_in ** -0.5

    def gain(k, shape):
        return 1.0 + 0.01 * jax.random.normal(k, shape, f32)

    L = DEPTH
    x = jax.random.normal(ks[0], (BATCH, SEQ, D_MODEL), f32)
    offsets = jax.random.randint(ks[1], (BATCH, 1), 0, 1024, dtype=jnp.int32)
    positions = (jnp.arange(SEQ, dtype=jnp.int32)[None, :] + offsets).astype(jnp.int32)
    return {
        "x": x,
        "positions": positions,
        "norm_mix_g": gain(ks[2], (L, D_MODEL)),
        "w_in": w(ks[3], (L, D_MODEL, D_IN), D_MODEL),
        "b_gate": 0.01 * jax.random.normal(ks[4], (L, N_BRANCH * D_MODEL), f32),
        "q_norm_g": gain(ks[5], (L, Q_LORA_RANK)),
        "w_uq": w(ks[6], (L, Q_LORA_RANK, N_HEADS * (QK_NOPE_DIM + QK_ROPE_DIM)), Q_LORA_RANK),
        "kv_norm_g": gain(ks[7], (L, KV_LORA_RANK)),
        "w_ukv": w(ks[8], (L, KV_LORA_RANK, N_HEADS * (QK_NOPE_DIM + V_HEAD_DIM)), KV_LORA_RANK),
        "w_o_attn": w(ks[9], (L, N_HEADS * V_HEAD_DIM, D_MODEL), N_HEADS * V_HEAD_DIM),
        "sgu_norm_g": gain(ks[10], (L, SGU_WIDTH)),
        "w_sgu": w(ks[11], (L, SGU_GROUPS, CHUNK, CHUNK), CHUNK),
        "b_sgu": gain(ks[12], (L, SGU_GROUPS, CHUNK)),
        "w_o_sgu": w(ks[13], (L, SGU_WIDTH, D_MODEL), SGU_WIDTH),
        "w_out": w(ks[14], (L, D_MODEL, D_MODEL), D_MODEL),
        "norm_ffn_g": gain(ks[15], (L, D_MODEL)),
        "w_gate_ffn": w(ks[16], (L, D_MODEL, D_FF), D_MODEL),
        "w_up_ffn": w(ks[17], (L, D_MODEL, D_FF), D_MODEL),
        "w_down_ffn": w(ks[18], (L, D_FF, D_MODEL), D_FF),
        "norm_final_g": gain(ks[19], (D_MODEL,)),
    }


def reference(x, positions, norm_mix_g, w_in, b_gate, q_norm_g, w_uq, kv_norm_g, w_ukv, w_o_attn,
              sgu_norm_g, w_sgu, b_sgu, w_o_sgu, w_out, norm_ffn_g, w_gate_ffn, w_up_ffn,
              w_down_ffn, norm_final_g):
    B, S, D = x.shape
    cos, sin = rope_tables(positions)
    o1 = Q_LORA_RANK
    o2 = o1 + KV_LORA_RANK
    o3 = o2 + QK_ROPE_DIM
    o4 = o3 + 2 * SGU_WIDTH
    h = x
    for l in range(DEPTH):
        a = rms_norm(h, norm_mix_g[l])
        z = a @ w_in[l]
        q_lat, kv_lat, k_pe = z[..., :o1], z[..., o1:o2], z[..., o2:o3]
        uv, gate_logits = z[..., o3:o4], z[..., o4:]
        y_attn = mla(q_lat, kv_lat, k_pe, cos, sin, q_norm_g[l], w_uq[l], kv_norm_g[l], w_ukv[l]) @ w_o_attn[l]
        y_sgu = sgu(uv, sgu_norm_g[l], w_sgu[l], b_sgu[l]) @ w_o_sgu[l]
        gates = jax.nn.sigmoid(gate_logits + b_gate[l]).reshape(B, S, N_BRANCH, D)
        merged = gates[:, :, 0, :] * y_attn + gates[:, :, 1, :] * y_sgu
        h = h + merged @ w_out[l]
        f = rms_norm(h, norm_ffn_g[l])
        h = h + (jax.nn.silu(f @ w_gate_ffn[l]) * (f @ w_up_ffn[l])) @ w_down_ffn[l]
    return rms_norm(h, norm_final_g)
```

```python
import math
import os
import numpy as np
import concourse.bass as bass
import concourse.mybir as mybir
from concourse.bass_utils import run_bass_kernel_spmd

F32 = mybir.dt.float32
BF16 = mybir.dt.bfloat16
I32 = mybir.dt.int32
AF = mybir.ActivationFunctionType
ALU = mybir.AluOpType

D = 2048
T = 1024
S = 2048
NH = 16
DFF = 5632
EPS = 1e-6
OQ, OKV, OKPE, OU, OV, OG0, OG1 = 0, 512, 1024, 1088, 2112, 3136, 5184
SCALE = 192.0 ** -0.5
NEG = -30000.0


class Op:
    __slots__ = ("eng", "fn", "deps", "signal", "sem", "val", "group", "dma", "needed", "name")

    def resolve(self):
        if self.group is not None:
            return self.group[-1]
        return self


class Buf:
    def __init__(self, name):
        self.name = name
        self.writers = {}
        self.readers = {}
        self.dsem = None
        self.dcount = 0


class Prog:
    ENGS = ("pe", "act", "dve", "pool", "sp")

    def __init__(self, nc):
        self.nc = nc
        self.ops = {e: [] for e in self.ENGS}
        self.sems = {}
        self.nops = 0
        self._sem_ctx = []

    def new_sem(self, name):
        ctx = self.nc.semaphore(name)
        s = ctx.__enter__()
        self._sem_ctx.append(ctx)
        return s

    def close(self):
        for c in reversed(self._sem_ctx):
            c.__exit__(None, None, None)

    def op(self, eng, fn, reads=(), writes=(), dma_into=None, dma_out=None, group=None,
           par=False, name=None, extra_deps=()):
        o = Op()
        o.eng = eng
        o.fn = fn
        o.signal = False
        o.needed = False
        o.group = group
        o.dma = (dma_into is not None) or (dma_out is not None)
        o.name = name
        o.sem = None
        o.val = None
        self.nops += 1
        deps = []
        seen = set()

        def add(d):
            if d is not None and id(d) not in seen and d is not o:
                seen.add(id(d))
                deps.append(d)

        wr = list(writes)
        if dma_into is not None:
            wr.append(dma_into)
        for b in reads:
            for d in b.writers.values():
                add(d)
        for b in wr:
            for d in b.readers.values():
                add(d)
            if not par:
                for d in b.writers.values():
                    add(d)
        for d in extra_deps:
            add(d)
        o.deps = deps
        for d in deps:
            d.needed = True
        key = ("dma", self.nops) if o.dma else eng
        for b in reads:
            b.readers[key] = o
        for b in wr:
            if par:
                b.writers[key] = o
            else:
                b.writers = {key: o}
                b.readers = {}
        sb = dma_into if dma_into is not None else dma_out
        if sb is not None:
            if sb.dsem is None:
                sb.dsem = self.new_sem("d_" + sb.name)
            sb.dcount += 1
            o.sem = sb.dsem
            o.val = 16 * sb.dcount
            o.signal = True
        if group is not None:
            group.append(o)
        self.ops[eng].append(o)
        return o

    def last(self, eng):
        for o in reversed(self.ops[eng]):
            if o.fn is not None and not o.dma:
                return o
        return None

    def barrier(self):
        lasts = {e: self.last(e) for e in ("pe", "act", "dve")}
        for e in ("pe", "act", "dve", "sp"):
            self.op(e, None, extra_deps=[lasts[x] for x in lasts])

    def finalize(self):
        for e in self.ENGS:
            for o in self.ops[e]:
                if o.needed and not o.dma:
                    o.resolve().signal = True
        for e in self.ENGS:
            sem = self.new_sem("e_" + e)
            self.sems[e] = sem
            cnt = 0
            for o in self.ops[e]:
                if o.dma:
                    continue
                if o.signal:
                    cnt += 1
                    o.sem = sem
                    o.val = cnt

    def emit(self, eng_name, eng):
        waited = {}
        for o in self.ops[eng_name]:
            need = {}
            for d in o.deps:
                r = d.resolve()
                if (not r.dma) and r.eng == "pe" and eng_name == "pe":
                    continue
                k = id(r.sem)
                if k not in need or need[k][1] < r.val:
                    need[k] = (r.sem, r.val)
            for k, (sem, val) in need.items():
                if waited.get(k, 0) < val:
                    eng.wait_ge(sem, val)
                    waited[k] = val
            if o.fn is None:
                continue
            ins = o.fn(eng)
            if o.signal:
                ins.then_inc(o.sem, 16 if o.dma else 1)

    def run_block(self, final_waits=()):
        self.finalize()
        nc = self.nc
        with nc.Block() as block:
            @block.tensor
            def _(e):
                self.emit("pe", e)

            @block.scalar
            def _(e):
                self.emit("act", e)

            @block.vector
            def _(e):
                self.emit("dve", e)

            @block.gpsimd
            def _(e):
                self.emit("pool", e)

            @block.sync
            def _(e):
                self.emit("sp", e)
                for o in final_waits:
                    e.wait_ge(o.sem, o.val)
        self.close()


WR = 0
CONST = 65536
PD = 75776
KB = 1024
TOTAL = PD + 133 * KB

DEBUG = os.environ.get("MK_DEBUG", "")


def build_program():
    nc = bass.Bass("TRN2", target_bir_lowering=False)

    def din(name, shape, dt=F32):
        return nc.dram_tensor(name, list(shape), dt, kind="ExternalInput").ap()

    x_own = din("x_own", [T, D])
    x_oth = din("x_oth", [T, D])
    pos_d = din("pos", [1, S], I32)
    ident_d = din("ident", [128, 128])
    tri_d = din("tri", [128, 128])
    omask_d = din("omask", [128, 128])
    mask01_d = din("mask01", [128, 128])
    invf_d = din("invf", [128, 1])
    sgn_d = din("sgn", [128, 1])
    norm_mix_g = din("norm_mix_g", [D])
    w_in = din("w_in", [D, 7232])
    b_gate = din("b_gate", [4096])
    q_norm_g = din("q_norm_g", [512])
    w_uq = din("w_uq", [512, 3072])
    kv_norm_g = din("kv_norm_g", [512])
    w_ukv = din("w_ukv", [512, 4096])
    w_o_attn = din("w_o_attn", [D, D])
    sgu_norm_g = din("sgu_norm_g", [1024])
    w_sgu = din("w_sgu", [8, 128, 128])
    b_sgu = din("b_sgu", [1024])
    w_o_sgu = din("w_o_sgu", [1024, D])
    w_out = din("w_out", [D, D])
    norm_ffn_g = din("norm_ffn_g", [D])
    w_gate = din("w_gate_ffn", [D, DFF])
    w_up = din("w_up_ffn", [D, DFF])
    w_down = din("w_down_ffn", [DFF, D])
    norm_final_g = din("norm_final_g", [D])
    out_d = nc.dram_tensor("out", [T, D], F32, kind="ExternalOutput").ap()
    dbg_d = {}

    P = Prog(nc)
    arena_ctx = nc.sbuf_tensor("arena", [128, TOTAL // 2], BF16)
    arena = arena_ctx.__enter__()
    ps_ctx = [nc.psum_tensor("ps%d" % i, [128, 512], F32) for i in range(8)]
    ps = [c.__enter__() for c in ps_ctx]
    Bps = [Buf("ps%d" % i) for i in range(8)]

    def view(off, cols, dt=BF16):
        esz = 2 if dt == BF16 else 4
        nb = cols * esz
        assert off % 4 == 0 and off + nb <= TOTAL, (off, nb)
        v = arena[:, off // 2: (off + nb) // 2]
        if dt != BF16:
            v = v.bitcast(dt)
        return v

    def pd(kb_off, cols, dt=BF16):
        return view(PD + int(kb_off * KB), cols, dt)

    g_bc = view(CONST, 2048, F32)
    ident = view(CONST + 8192, 128)
    tri = view(CONST + 8448, 128)
    omask = view(CONST + 8704, 128)
    bg = view(CONST + 8960, 32, F32)
    invf = view(CONST + 9088, 1, F32)
    sgn = view(CONST + 9120, 1, F32)
    stats = view(CONST + 9152, 64, F32)
    Bgbc, Bcst, Bsmall = Buf("gbc"), Buf("cst"), Buf("small")
    Bst = [Buf("st%d" % i) for i in range(16)]
    st_ctr = [0]

    def stat():
        i = st_ctr[0] % 16
        st_ctr[0] += 1
        return stats[:, 4 * i: 4 * i + 1], stats[:, 4 * i + 1: 4 * i + 2], Bst[i]

    NSLOT = 4
    Bw = [Buf("w%d" % i) for i in range(NSLOT)]
    wslot = [view(WR + i * 16384, 8192) for i in range(NSLOT)]
    wctr = [0]

    def wload(pieces, extra=()):
        s = wctr[0] % NSLOT
        wctr[0] += 1
        for i, (dfn, src) in enumerate(pieces):
            P.op("pool", lambda e, d=dfn(wslot[s]), sr=src: e.dma_start(out=d, in_=sr),
                 dma_into=Bw[s], par=(i > 0), extra_deps=(extra if i == 0 else ()))
        return wslot[s], Bw[s]

    def wtile_cols(w_ap, c0, ncols, kc=16, extra=()):
        src = w_ap.rearrange("(c p) n -> p c n", p=128)[:, :, c0:c0 + ncols]
        sl, b = wload([(lambda a: a[:, 0:kc * ncols].rearrange("p (c n) -> p c n", c=kc), src)], extra=extra)
        return sl[:, 0:kc * ncols].rearrange("p (c n) -> p c n", c=kc), b

    bank_ctr = [0]
    bank_set = [list(range(8))]

    def bank():
        lst = bank_set[0]
        i = lst[bank_ctr[0] % len(lst)]
        bank_ctr[0] += 1
        return i

    def mm_group(bi, items, reads):
        grp = []
        n = len(items)
        for i, (o_ap, l, r) in enumerate(items):
            P.op("pe", lambda e, o_ap=o_ap, l=l, r=r, i=i: e.matmul(o_ap, l, r, start=(i == 0), stop=(i == n - 1)),
                 reads=reads, writes=[Bps[bi]], group=grp)
        return grp

    def transposes(bi, srcs, reads):
        grp = []
        for i, sap in enumerate(srcs):
            P.op("pe", lambda e, sap=sap, i=i: e.matmul(ps[bi][:, i * 128:(i + 1) * 128], sap, ident, start=True, stop=True),
                 reads=reads + [Bcst], writes=[Bps[bi]], group=grp)
        return grp

    ev_ctr = [0]

    def copy_evac(out_ap, in_ap, reads, writes, par=True, eng=None):
        if eng is None:
            eng = ("act", "dve")[ev_ctr[0] % 2]
            ev_ctr[0] += 1
        if eng == "act":
            return P.op("act", lambda e: e.copy(out=out_ap, in_=in_ap), reads=reads, writes=writes, par=par)
        return P.op("dve", lambda e: e.tensor_copy(out=out_ap, in_=in_ap), reads=reads, writes=writes, par=par)

    def rms_rstd(src_ap, src_bufs, n, junk_ap, Bjunk, par_junk=False):
        a0, a1, b = stat()
        P.op("act", lambda e: e.activation(out=junk_ap, in_=src_ap, func=AF.Square, scale=float(n) ** -0.5, accum_out=a0),
             reads=src_bufs, writes=[b])
        P.ops["act"][-1].deps.extend(d for d in (list(Bjunk.readers.values()) + list(Bjunk.writers.values()))
                                     if d not in P.ops["act"][-1].deps and d is not P.ops["act"][-1])
        for d in P.ops["act"][-1].deps:
            d.needed = True
        if par_junk:
            Bjunk.writers["act"] = P.ops["act"][-1]
        else:
            Bjunk.writers = {"act": P.ops["act"][-1]}
            Bjunk.readers = {}
        P.op("act", lambda e: e.activation(out=a1, in_=a0, func=AF.Sqrt, bias=epsb, scale=1.0), reads=[b, Bsmall], writes=[b])
        P.op("dve", lambda e: e.reciprocal(out=a1, in_=a1), reads=[b], writes=[b])
        return a1, b

    def dump(name, ap, bufs, shape):
        if DEBUG and name in DEBUG.split(","):
            dt = ap.dtype
            dd = nc.dram_tensor("dbg_" + name, list(shape), dt, kind="ExternalOutput").ap()
            dbg_d[name] = P.op("sp", lambda e: e.dma_start(out=dd, in_=ap), reads=bufs, dma_out=Buf("dbg_" + name))

    epsb = view(CONST + 9152 + 256, 1, F32)
    P.op("pool", lambda e: e.dma_start(out=ident, in_=ident_d), dma_into=Bcst)
    P.op("pool", lambda e: e.dma_start(out=tri, in_=tri_d), dma_into=Bcst, par=True)
    P.op("pool", lambda e: e.dma_start(out=omask, in_=omask_d), dma_into=Bcst, par=True)
    P.op("sp", lambda e: e.dma_start(out=g_bc, in_=norm_mix_g.partition_broadcast(128)), dma_into=Bgbc)
    P.op("sp", lambda e: e.dma_start(out=invf, in_=invf_d), dma_into=Bsmall)
    P.op("sp", lambda e: e.dma_start(out=sgn, in_=sgn_d), dma_into=Bsmall, par=True)
    P.op("dve", lambda e: e.memset(epsb, EPS), writes=[Bsmall], par=True)

    CC = pd(32, 2048, F32)
    SS = pd(40, 2048, F32)
    Btab = Buf("tab")
    TWO_PI = 2.0 * math.pi
    Btt = Buf("tabtmp")

    def table_gen():
        for hf in range(2):
            tsl = slice(hf * 1024, (hf + 1) * 1024)
            posi = pd(88, 1024, I32)
            posf = pd(92, 1024, F32)
            ang = pd(96, 1024, F32)
            kf = pd(100, 1024, F32)
            ki = pd(104, 1024, I32)
            Bt = Btt
            P.op("sp", lambda e, posi=posi, tsl=tsl: e.dma_start(out=posi, in_=pos_d[:, tsl].partition_broadcast(128)),
                 dma_into=Bt)
            P.op("dve", lambda e, posi=posi, posf=posf: e.tensor_copy(out=posf, in_=posi), reads=[Bt], writes=[Bt])
            yield
            for which in range(2):
                P.op("dve", lambda e, which=which: e.tensor_scalar(out=ang, in0=posf, scalar1=invf[:, 0:1],
                                                                     scalar2=(0.5 * math.pi if which else 0.0),
                                                                     op0=ALU.mult, op1=ALU.add),
                     reads=[Bt, Bsmall], writes=[Bt])
                yield
                P.op("dve", lambda e: e.tensor_scalar(out=kf, in0=ang, scalar1=1.0 / TWO_PI, scalar2=None, op0=ALU.mult),
                     reads=[Bt], writes=[Bt])
                yield
                P.op("dve", lambda e: e.tensor_copy(out=ki, in_=kf), reads=[Bt], writes=[Bt])
                yield
                P.op("dve", lambda e: e.tensor_copy(out=kf, in_=ki), reads=[Bt], writes=[Bt])
                yield
                P.op("dve", lambda e: e.scalar_tensor_tensor(out=ang, in0=kf, scalar=-TWO_PI, in1=ang, op0=ALU.mult, op1=ALU.add),
                     reads=[Bt], writes=[Bt])
                yield
                P.op("dve", lambda e: e.tensor_scalar(out=kf, in0=ang, scalar1=math.pi, scalar2=-TWO_PI, op0=ALU.is_gt, op1=ALU.mult),
                     reads=[Bt], writes=[Bt])
                yield
                P.op("dve", lambda e: e.tensor_tensor(out=ang, in0=ang, in1=kf, op=ALU.add), reads=[Bt], writes=[Bt])
                yield
                dst = (CC if which else SS)[:, tsl]
                P.op("act", lambda e, dst=dst: e.activation(out=dst, in_=ang, func=AF.Sin), reads=[Bt], writes=[Btab], par=True)
                if not which:
                    P.op("dve", lambda e, dst=dst: e.tensor_scalar(out=dst, in0=dst, scalar1=sgn[:, 0:1], scalar2=None, op0=ALU.mult),
                         reads=[Btab, Bsmall], writes=[Btab])
                yield

    tgen = [table_gen()]

    def tick(n=2):
        for _ in range(n):
            if tgen[0] is None:
                return
            try:
                next(tgen[0])
            except StopIteration:
                tgen[0] = None


    aT = pd(0, 16 * 1024).rearrange("p (c n) -> p c n", c=16)
    BaT = [Buf("aT%d" % i) for i in range(8)]

    x_dma_ops = []

    def phase1_a(t, x_d, xs_list, xns, Bxs, Bxns, gbuf=Bgbc, hsrc=None, junk=None):
        if hsrc is None:
            xs = xs_list[t % 2]
            bx = Bxs[t % 2]
            x_dma_ops.append(P.op("sp", lambda e, xs=xs, t=t: e.dma_start(out=xs, in_=x_d[t * 128:(t + 1) * 128, :]), dma_into=bx))
        else:
            xs, bx = hsrc[t]
        xn, Bxn = xns[t % len(xns)], Bxns[t % len(xns)]
        if junk is None:
            rstd, brs = rms_rstd(xs, [bx], D, xn, Bxn)
        else:
            rstd, brs = rms_rstd(xs.rearrange("p (c n) -> p c n", c=16), [bx], D, junk[0], junk[1], par_junk=True)
        P.op("dve", lambda e, xs=xs, rstd=rstd, xn=xn: e.scalar_tensor_tensor(out=xn, in0=xs, scalar=rstd, in1=g_bc,
                                                                             op0=ALU.mult, op1=ALU.mult),
             reads=[bx, brs, gbuf], writes=[Bxn])

    def phase1_b(t, xns, Bxns, dst=aT, Bdst=BaT):
        xn, Bxn = xns[t % len(xns)], Bxns[t % len(xns)]
        for cg in range(4):
            bi = bank()
            transposes(bi, [xn[:, (4 * cg + i) * 128:(4 * cg + i + 1) * 128] for i in range(4)], [Bxn])
            copy_evac(dst[:, 4 * cg:4 * cg + 4, t * 128:(t + 1) * 128],
                      ps[bi][:, :].rearrange("p (c n) -> p c n", c=4), [Bps[bi]], [Bdst[t]],
                      eng=("act" if (tgen[0] is not None or cg < 3) else "dve"))

    def phase1_tile(t, x_d, xs_list, xns, Bxs, Bxns, gbuf=Bgbc, dst=aT, Bdst=BaT, hsrc=None):
        phase1_a(t, x_d, xs_list, xns, Bxs, Bxns, gbuf, hsrc)
        phase1_b(t, xns, Bxns, dst, Bdst)

    def phase1(x_d, xs_list, xns, Bxs, Bxns, gbuf=Bgbc, dst=aT, Bdst=BaT, hsrc=None, ticks=0):
        assert len(xns) >= 2
        for t in range(8):
            phase1_a(t, x_d, xs_list, xns, Bxs, Bxns, gbuf, hsrc)
            if t >= 1:
                phase1_b(t - 1, xns, Bxns, dst, Bdst)
            tick(ticks)
        phase1_b(7, xns, Bxns, dst, Bdst)

    kvnT = pd(64, 4 * 2048).rearrange("p (c n) -> p c n", c=4)
    qnT = pd(80, 4 * 1024).rearrange("p (c n) -> p c n", c=4)
    kpeT = pd(88, 2048)
    qpeT = pd(92, 8 * 1024).rearrange("p (h n) -> p h n", h=8)
    BkvnT = [Buf("kvnT%d" % i) for i in range(16)]
    BqnT = [Buf("qnT%d" % i) for i in range(8)]
    BkpeT = [Buf("kpeT%d" % i) for i in range(4)]
    BqpeT = [Buf("qpeT%d" % i) for i in range(8)]

    xs_a = [pd(48, 2048, F32), pd(56, 2048, F32)]
    Bxs_a = [Buf("xs0"), Buf("xs1")]
    xn_a = [pd(108, 2048), pd(112, 2048)]
    Bxn_a = [Buf("xn0"), Buf("xn1")]
    junk_a = pd(130, 512)
    Bjunk_a = Buf("junk")
    gkv_bc = pd(116, 512, F32)
    gq_bc = pd(118, 512, F32)
    lat = [pd(120, 512), pd(121, 512)]
    Blat = [Buf("lat0"), Buf("lat1")]
    rt = [pd(122, 512, F32), pd(124, 512, F32)]
    Brt = [Buf("rt%d" % i) for i in range(2)]
    kpeB = pd(126, 2048)
    Bgsm = Buf("gsm")
    P.op("sp", lambda e: e.dma_start(out=gkv_bc, in_=kv_norm_g.partition_broadcast(128)), dma_into=Bgsm)
    P.op("sp", lambda e: e.dma_start(out=gq_bc, in_=q_norm_g.partition_broadcast(128)), dma_into=Bgsm, par=True)

    lat_ctr = [0]

    def latent_a(t, W, Bwt, gs_bc):
        bi = bank()
        mm_group(bi, [(ps[bi][:, :], aT[:, c, t * 128:(t + 1) * 128], W[:, c, :]) for c in range(16)], [BaT[t], Bwt])
        rstd, brs = rms_rstd(ps[bi][:, :], [Bps[bi]], 512, junk_a, Bjunk_a)
        li = lat_ctr[0] % 2
        lat_ctr[0] += 1
        P.op("dve", lambda e: e.scalar_tensor_tensor(out=lat[li], in0=ps[bi][:, :], scalar=rstd, in1=gs_bc,
                                                    op0=ALU.mult, op1=ALU.mult),
             reads=[Bps[bi], brs, Bgsm], writes=[Blat[li]])
        return li

    def latent_b(li, outT, Bout, tok0):
        b2 = bank()
        transposes(b2, [lat[li][:, i * 128:(i + 1) * 128] for i in range(4)], [Blat[li]])
        copy_evac(outT[:, :, tok0:tok0 + 128], ps[b2][:, :].rearrange("p (c n) -> p c n", c=4), [Bps[b2]], [Bout])

    def latents(W, Bwt, gs_bc, outT, Bouts, tok_base):
        prev = None
        for t in range(8):
            li = latent_a(t, W, Bwt, gs_bc)
            tick(2)
            if prev is not None:
                latent_b(prev[0], outT, Bouts[prev[1]], tok_base + prev[1] * 128)
            prev = (li, t)
        latent_b(prev[0], outT, Bouts[prev[1]], tok_base + prev[1] * 128)

    rope_ctr = [0]

    def rope_proj(items_n, items_s, reads, tab0, outs, Bout):
        while tgen[0] is not None:
            tick(1)
        ba = bank()
        mm_group(ba, [(ps[ba][:, :], l, r) for (l, r) in items_n], reads)
        bb = bank()
        mm_group(bb, [(ps[bb][:, :], l, r) for (l, r) in items_s], reads)
        t1, t2 = rt[0], rt[1]
        P.op("dve", lambda e: e.tensor_tensor(out=t1, in0=ps[ba][:, :], in1=CC[:, tab0:tab0 + 512], op=ALU.mult),
             reads=[Bps[ba], Btab], writes=[Brt[0]])
        P.op("dve", lambda e: e.tensor_tensor(out=t2, in0=ps[bb][:, :], in1=SS[:, tab0:tab0 + 512], op=ALU.mult),
             reads=[Bps[bb], Btab], writes=[Brt[1]])
        for (psl, o_ap) in outs:
            P.op("dve", lambda e, psl=psl, o_ap=o_ap: e.tensor_tensor(out=o_ap, in0=t1[psl, :], in1=t2[psl, :], op=ALU.add),
                 reads=[Brt[0], Brt[1]], writes=[Bout], par=True)

    w_in_v = w_in.rearrange("(c p) n -> p c n", p=128)

    def kpe_dst(a, o, n):
        return a[:, 0:16 * 256].rearrange("p (c n) -> p c n", c=16)[:, :, o:o + n]
    WP = {}

    def load_p2a_weights():
        if WP:
            return
        ex = [x_dma_ops[min(5, len(x_dma_ops) - 1)]]
        WP["kv"] = wtile_cols(w_in, OKV, 512, extra=ex)
        slk, BWkpe_ = wload([
            (lambda a: kpe_dst(a, 0, 64), w_in_v[:, :, OKPE:OKPE + 64]),
            (lambda a: kpe_dst(a, 64, 64), w_in_v[:, :, OKPE:OKPE + 64]),
            (lambda a: kpe_dst(a, 128, 32), w_in_v[:, :, OKPE + 32:OKPE + 64]),
            (lambda a: kpe_dst(a, 160, 32), w_in_v[:, :, OKPE:OKPE + 32]),
            (lambda a: kpe_dst(a, 192, 32), w_in_v[:, :, OKPE + 32:OKPE + 64]),
            (lambda a: kpe_dst(a, 224, 32), w_in_v[:, :, OKPE:OKPE + 32]),
        ])
        WP["kpe"] = (slk[:, 0:16 * 256].rearrange("p (c n) -> p c n", c=16), BWkpe_)

    Bkz = Buf("kz")
    P.op("dve", lambda e: e.memset(kpeB[0:64, :], 0.0), writes=[Bkz])

    def do_tokens(x_d, tok_base, with_q):
        phase1(x_d, xs_a, xn_a, Bxs_a, Bxn_a, ticks=2)
        load_p2a_weights()
        Wkv, BWkv = WP["kv"]
        Wkpe, BWkpe = WP["kpe"]
        latents(Wkv, BWkv, gkv_bc, kvnT, BkvnT[tok_base // 128:tok_base // 128 + 8], tok_base)
        for tg in range(2):
            tok0 = tok_base + tg * 512
            rope_proj([(Wkpe[:, c, 0:128], aT[:, c, tg * 512:(tg + 1) * 512]) for c in range(16)],
                      [(Wkpe[:, c, 128:256], aT[:, c, tg * 512:(tg + 1) * 512]) for c in range(16)],
                      [BWkpe] + BaT[4 * tg:4 * tg + 4], tok0,
                      [(slice(0, 64), kpeT[0:64, tok0:tok0 + 512]), (slice(64, 128), kpeB[64:128, tok0:tok0 + 512])],
                      BkpeT[tok0 // 512])

    do_tokens(x_oth, 1024, False)
    P.op("dve", lambda e: e.memset(kpeT[64:128, :], 0.0), reads=[Btab], writes=[Bkz], par=True)
    do_tokens(x_own, 0, True)
    Wq, BWq = wtile_cols(w_in, OQ, 512)
    latents(Wq, BWq, gq_bc, qnT, BqnT, 0)
    w_uq_v = w_uq.rearrange("(c p) (h e) -> p c h e", p=128, e=192)

    def qd(a, base, eo, n):
        v = a[:, 0:4 * 2048].rearrange("p (c s h e) -> p c s h e", c=4, s=2, h=16)
        return v[:, :, base, :, eo:eo + n]
    qpieces = []
    for c_ in range(4):
        qpieces += [
            (lambda a, c_=c_: qd(a, 0, 0, 64)[:, c_], w_uq_v[:, c_, :, 128:192]),
            (lambda a, c_=c_: qd(a, 1, 0, 32)[:, c_], w_uq_v[:, c_, :, 160:192]),
            (lambda a, c_=c_: qd(a, 1, 32, 32)[:, c_], w_uq_v[:, c_, :, 128:160]),
        ]
    slq, BWqpe = wload(qpieces)
    Wqpe = slq[:, 0:4 * 2048].rearrange("p (c s n) -> p c s n", c=4, s=2)
    for hp in range(8):
        for tg in range(2):
            rope_proj([(Wqpe[:, c, 0, hp * 128:(hp + 1) * 128], qnT[:, c, tg * 512:(tg + 1) * 512]) for c in range(4)],
                      [(Wqpe[:, c, 1, hp * 128:(hp + 1) * 128], qnT[:, c, tg * 512:(tg + 1) * 512]) for c in range(4)],
                      [BWqpe] + BqnT[4 * tg:4 * tg + 4], tg * 512,
                      [(slice(0, 128), qpeT[:, hp, tg * 512:(tg + 1) * 512])], BqpeT[hp])
    dump("kvnT", pd(64, 4 * 2048), BkvnT, [128, 4 * 2048])
    dump("qnT", pd(80, 4 * 1024), BqnT, [128, 4 * 1024])
    dump("kpeT", kpeT, BkpeT, [128, 2048])
    dump("qpeT", pd(92, 8 * 1024), BqpeT, [128, 8 * 1024])
    P.barrier()

    attnT = pd(32, 16 * 1024).rearrange("p (h n) -> p h n", h=16)
    BattnT = [Buf("attnT%d" % i) for i in range(16)]
    knT = pd(0, 4 * 2048).rearrange("p (h n) -> p h n", h=4)
    qnopeT = pd(16, 4 * 1024).rearrange("p (h n) -> p h n", h=4)
    v1 = pd(108, 16 * 4 * 129).rearrange("p (t h d) -> p t h d", t=16, h=4)
    BknT = [Buf("knT%d" % i) for i in range(4)]
    Bqno = [Buf("qno%d" % i) for i in range(4)]
    Bv1 = Buf("v1")
    pT = [pd(24, 512), pd(25, 512), pd(26, 512), pd(27, 512)]
    BpT = [Buf("pT%d" % i) for i in range(4)]
    atok = [pd(28, 128), pd(28.25, 128), pd(28.5, 128), pd(28.75, 128)]
    Batok = [Buf("atok%d" % i) for i in range(4)]
    rsb = pd(29, 8, F32)
    sctr3 = [0]
    Brs = [Buf("rs%d" % i) for i in range(8)]
    P.op("dve", lambda e: e.memset(pd(108, 16 * 4 * 129), 1.0), writes=[Bv1])
    pctr = [0]
    actr = [0]
    w_ukv_v = w_ukv.rearrange("(c p) n -> p c n", p=128)
    w_uq_v2 = w_uq.rearrange("(c p) n -> p c n", p=128)
    for G in range(4):
        slg, BWg = wload([
            (lambda a: a[:, 0:4096].rearrange("p (c n) -> p c n", c=4), w_ukv_v[:, :, G * 1024:(G + 1) * 1024]),
            (lambda a: a[:, 4096:4096 + 4 * 768].rearrange("p (c n) -> p c n", c=4), w_uq_v2[:, :, G * 768:(G + 1) * 768]),
        ])
        Wkv4 = slg[:, 0:4096].rearrange("p (c h t d) -> p c h t d", c=4, h=4, t=2)
        Wq4 = slg[:, 4096:4096 + 4 * 768].rearrange("p (c h e) -> p c h e", c=4, h=4)
        bank_set[0] = [0, 1, 2, 3]
        for hl in range(4):
            for tg in range(4):
                bi = bank()
                mm_group(bi, [(ps[bi][:, :], Wkv4[:, c, hl, 0, :], kvnT[:, c, tg * 512:(tg + 1) * 512]) for c in range(4)],
                         [BWg] + BkvnT[4 * tg:4 * tg + 4])
                copy_evac(knT[:, hl, tg * 512:(tg + 1) * 512], ps[bi][:, :], [Bps[bi]], [BknT[hl]])
            for tg in range(2):
                bi = bank()
                mm_group(bi, [(ps[bi][:, :], Wq4[:, c, hl, 0:128], qnT[:, c, tg * 512:(tg + 1) * 512]) for c in range(4)],
                         [BWg] + BqnT[4 * tg:4 * tg + 4])
                copy_evac(qnopeT[:, hl, tg * 512:(tg + 1) * 512], ps[bi][:, :], [Bps[bi]], [Bqno[hl]])
        for kt in range(16):
            bi = bank()
            mm_group(bi, [(ps[bi][:, :].rearrange("p (h d) -> p h d", h=4), kvnT[:, c, kt * 128:(kt + 1) * 128], Wkv4[:, c, :, 1, :])
                          for c in range(4)], [BWg, BkvnT[kt]])
            copy_evac(v1[:, kt, :, 0:128], ps[bi][:, :].rearrange("p (h d) -> p h d", h=4), [Bps[bi]], [Bv1])
        steps = []
        for hl in range(4):
            for qg in range(2):
                j0 = 4 * qg
                for i in range(j0 + 4):
                    for typ in range(2):
                        steps.append((hl, qg, i, typ))
        LAG = 2
        info = {}
        pend_tr = []

        def flush_tr(now, age=2):
            while pend_tr and (now is None or now - pend_tr[0][3] >= age):
                ai_, h_, j_, _ = pend_tr.pop(0)
                transposes(3, [atok[ai_]], [Batok[ai_]])
                copy_evac(attnT[:, h_, j_ * 128:(j_ + 1) * 128], ps[3][:, 0:128], [Bps[3]], [BattnT[h_]], eng="dve")

        for idx in range(len(steps) + LAG):
            if idx < len(steps):
                hl, qg, i, typ = steps[idx]
                h = 4 * G + hl
                hp, half = h // 2, (h % 2) * 64
                j0 = 4 * qg
                kt = i + 8 * typ
                js = max(i, j0)
                nq = (j0 + 4 - js) * 128
                q0 = js * 128
                bi = sctr3[0] % 3
                sctr3[0] += 1
                items = [(ps[bi][:, 0:nq], knT[:, hl, kt * 128:(kt + 1) * 128], qnopeT[:, hl, q0:q0 + nq]),
                         (ps[bi][:, 0:nq], (kpeB if (h % 2) else kpeT)[:, kt * 128:(kt + 1) * 128], qpeT[:, hp, q0:q0 + nq])]
                if i >= j0:
                    items.append((ps[bi][:, 0:128], ident, omask if typ else tri))
                mm_group(bi, items, [BknT[hl], Bqno[hl], BkpeT[kt // 4], BqpeT[hp], Bcst, Bkz])
                pi = pctr[0] % 4
                pctr[0] += 1
                P.op("act", lambda e, bi=bi, pi=pi, nq=nq: e.activation(out=pT[pi][:, 0:nq], in_=ps[bi][:, 0:nq],
                                                                        func=AF.Exp, scale=SCALE),
                     reads=[Bps[bi]], writes=[BpT[pi]])
                info[idx] = pi
            if idx >= LAG:
                flush_tr(idx)
                hl, qg, i, typ = steps[idx - LAG]
                pi = info.pop(idx - LAG)
                h = 4 * G + hl
                j0 = 4 * qg
                kt = i + 8 * typ
                js = max(i, j0)
                for j in range(js, j0 + 4):
                    ab = 4 + (j - j0)
                    first = (i == 0 and typ == 0)
                    lastk = (i == j and typ == 1)
                    P.op("pe", lambda e, ab=ab, pi=pi, j=j, js=js, kt=kt, hl=hl, first=first, lastk=lastk:
                         e.matmul(ps[ab][:, 0:129], pT[pi][:, (j - js) * 128:(j - js + 1) * 128], v1[:, kt, hl, :],
                                  start=first, stop=lastk),
                         reads=[BpT[pi], Bv1], writes=[Bps[ab]])
                    if lastk:
                        ri = j
                        P.op("dve", lambda e, ab=ab, ri=ri: e.reciprocal(out=rsb[:, ri:ri + 1], in_=ps[ab][:, 128:129]),
                             reads=[Bps[ab]], writes=[Brs[ri]])
                        ai = actr[0] % 4
                        actr[0] += 1
                        P.op("dve", lambda e, ab=ab, ri=ri, ai=ai: e.tensor_scalar(out=atok[ai], in0=ps[ab][:, 0:128],
                                                                                  scalar1=rsb[:, ri:ri + 1], scalar2=None,
                                                                                  op0=ALU.mult),
                             reads=[Bps[ab], Brs[ri]], writes=[Batok[ai]])
                        pend_tr.append((ai, h, j, idx))
        flush_tr(None)
    bank_set[0] = list(range(8))
    dump("attnT", pd(32, 16 * 1024), BattnT, [128, 16 * 1024])
    P.barrier()

    xs_b = [pd(64, 2048, F32), pd(72, 2048, F32)]
    Bxs_b = [Buf("xsb0"), Buf("xsb1")]
    xn_b = [pd(80, 2048), pd(84, 2048)]
    Bxn_b = [Buf("xnb0"), Buf("xnb1")]
    for t in range(4):
        phase1_a(t, x_own, xs_b, xn_b, Bxs_b, Bxn_b)
        if t >= 1:
            phase1_b(t - 1, xn_b, Bxn_b)
    sguT = pd(96, 8 * 1024).rearrange("p (g n) -> p g n", g=8)
    BsguT = [Buf("sguT%d" % i) for i in range(8)]
    gsgu_bc = pd(112, 1024, F32)
    bsgu_bc = pd(116, 1024, F32)
    wsT = pd(120, 8 * 128).rearrange("p (g n) -> p g n", g=8)
    mask01 = pd(122, 128, F32)
    wraw = pd(122.5, 8 * 128, F32).rearrange("p (g n) -> p g n", g=8)
    wrawb = pd(126.5, 8 * 128).rearrange("p (g n) -> p g n", g=8)
    tmpm = [pd(128.5, 512, F32)]
    Btmpm = [Buf("tmpm0")]
    vg = [pd(88, 1024, F32), pd(92, 1024, F32)]
    vn = [pd(130.5, 1024)]
    Bvg = [Buf("vg0"), Buf("vg1")]
    Bvn = [Buf("vn0")]
    Bsg = Buf("sgc")
    BwsT, Bwraw, Bwrawb = Buf("wsT"), Buf("wraw"), Buf("wrawb")
    Bbg = Buf("bgate")
    P.op("sp", lambda e: e.dma_start(out=bg, in_=b_gate.rearrange("(c p) -> p c", p=128), allow_slow_non_contiguous=True),
         dma_into=Bbg)
    P.op("sp", lambda e: e.dma_start(out=gsgu_bc, in_=sgu_norm_g.partition_broadcast(128)), dma_into=Bsg)
    P.op("sp", lambda e: e.dma_start(out=bsgu_bc, in_=b_sgu.partition_broadcast(128)), dma_into=Bsg, par=True)
    P.op("sp", lambda e: e.dma_start(out=mask01, in_=mask01_d), dma_into=Bsg, par=True)
    Wus = [wtile_cols(w_in, OU + half * 512, 512) for half in range(2)]

    def u_proj(g, tg):
        Wu, BWu = Wus[g // 4]
        gl = g % 4
        bi = bank()
        mm_group(bi, [(ps[bi][:, :], Wu[:, c, gl * 128:(gl + 1) * 128], aT[:, c, tg * 512:(tg + 1) * 512]) for c in range(16)],
                 [BWu] + BaT[4 * tg:4 * tg + 4])
        P.op("act", lambda e, bi=bi, g=g, tg=tg: e.activation(out=sguT[:, g, tg * 512:(tg + 1) * 512], in_=ps[bi][:, :],
                                                              func=AF.Gelu_apprx_tanh),
             reads=[Bps[bi]], writes=[BsguT[g]], par=True)

    for k in range(4):
        phase1_a(4 + k, x_own, xs_b, xn_b, Bxs_b, Bxn_b)
        phase1_b(3 + k, xn_b, Bxn_b)
        u_proj(2 * k, 0)
        u_proj(2 * k + 1, 0)
    phase1_b(7, xn_b, Bxn_b)
    for g in range(8):
        u_proj(g, 1)
    P.op("sp", lambda e: e.dma_start(out=wraw, in_=w_sgu.rearrange("g t s -> t g s")), dma_into=Bwraw)
    P.op("dve", lambda e: e.tensor_copy(out=wrawb, in_=wraw), reads=[Bwraw], writes=[Bwrawb])
    for gh in range(2):
        bi = bank()
        transposes(bi, [wrawb[:, gh * 4 + i, :] for i in range(4)], [Bwrawb])
        for i in range(4):
            P.op("dve", lambda e, bi=bi, gh=gh, i=i: e.tensor_tensor(out=wsT[:, gh * 4 + i, :], in0=ps[bi][:, i * 128:(i + 1) * 128],
                                                                    in1=mask01, op=ALU.mult),
                 reads=[Bps[bi], Bsg], writes=[BwsT], par=True)
    Wv = [wtile_cols(w_in, OV + half * 512, 512) for half in range(2)]
    def sgu_v(t):
        vi = t % 2
        bis = []
        for half in range(2):
            bi = bank()
            bis.append(bi)
            mm_group(bi, [(ps[bi][:, :], aT[:, c, t * 128:(t + 1) * 128], Wv[half][0][:, c, :]) for c in range(16)],
                     [BaT[t], Wv[half][1]])
        for half in range(2):
            bi = bis[half]
            P.op("act", lambda e, bi=bi, vi=vi, half=half: e.activation(out=vg[vi][:, half * 512:(half + 1) * 512], in_=ps[bi][:, :],
                                                                        func=AF.Gelu_apprx_tanh),
                 reads=[Bps[bi]], writes=[Bvg[vi]], par=(half > 0))

    def sgu_chain(t):
        vi = t % 2
        rstd, brs = rms_rstd(vg[vi], [Bvg[vi]], 1024, vn[0], Bvn[0])
        P.op("dve", lambda e, vi=vi, rstd=rstd: e.scalar_tensor_tensor(out=vn[0], in0=vg[vi], scalar=rstd, in1=gsgu_bc,
                                                                      op0=ALU.mult, op1=ALU.mult),
             reads=[Bvg[vi], brs, Bsg], writes=[Bvn[0]])

    def sgu_mix(t):
        for gh in range(2):
            bi = bank()
            grp = []
            for gl in range(4):
                g = gh * 4 + gl
                P.op("pe", lambda e, bi=bi, gl=gl, g=g: e.matmul(ps[bi][:, gl * 128:(gl + 1) * 128], vn[0][:, g * 128:(g + 1) * 128],
                                                                  wsT[:, g, :], start=True, stop=True),
                     reads=[Bvn[0], BwsT], writes=[Bps[bi]], group=grp)
            P.op("dve", lambda e, bi=bi, gh=gh: e.tensor_tensor(out=tmpm[0], in0=ps[bi][:, :], in1=bsgu_bc[:, gh * 512:(gh + 1) * 512],
                                                               op=ALU.add),
                 reads=[Bps[bi], Bsg], writes=[Btmpm[0]])
            o_ap = sguT[:, gh * 4:gh * 4 + 4, t * 128:(t + 1) * 128]
            P.op("dve", lambda e, o_ap=o_ap: e.tensor_tensor(out=o_ap, in0=o_ap, in1=tmpm[0][:, :].rearrange("p (g n) -> p g n", g=4),
                                                             op=ALU.mult),
                 reads=[Btmpm[0]] + BsguT[gh * 4:gh * 4 + 4], writes=BsguT[gh * 4:gh * 4 + 4], par=True)

    for t in range(8):
        sgu_v(t)
        if t >= 1:
            sgu_mix(t - 1)
        sgu_chain(t)
    sgu_mix(7)
    dump("sguT", pd(96, 8 * 1024), BsguT, [128, 8 * 1024])

    mergedT = pd(64, 16 * 1024).rearrange("p (c n) -> p c n", c=16)
    BmT = [Buf("mT%d" % i) for i in range(16)]
    sgt = [pd(112, 512, F32), pd(114, 512, F32)]
    Bsgt = [Buf("sgt0"), Buf("sgt1")]
    m1t = [pd(116, 512, F32), pd(118, 512, F32)]
    Bm1t = [Buf("m1t0"), Buf("m1t1")]
    sctr = [0]
    for pas in range(2):
        for cg in range(4):
            Wg, BWg_ = wtile_cols(w_in, (OG0 if pas == 0 else OG1) + cg * 512, 512)
            if pas == 0:
                Wo, BWo = wtile_cols(w_o_attn, cg * 512, 512)
                KO, srcT, Bsrc = 16, attnT, BattnT
            else:
                Wo, BWo = wtile_cols(w_o_sgu, cg * 512, 512, kc=8)
                KO, srcT, Bsrc = 8, sguT, BsguT
            for c4 in range(4):
                c = cg * 4 + c4
                for tg in range(2):
                    tsl = slice(tg * 512, (tg + 1) * 512)
                    bgt = bank()
                    mm_group(bgt, [(ps[bgt][:, :], Wg[:, k, c4 * 128:(c4 + 1) * 128], aT[:, k, tsl]) for k in range(16)],
                             [BWg_] + BaT[4 * tg:4 * tg + 4])
                    by = bank()
                    mm_group(by, [(ps[by][:, :], Wo[:, k, c4 * 128:(c4 + 1) * 128], srcT[:, k, tsl]) for k in range(KO)],
                             [BWo] + Bsrc)
                    si = sctr[0] % 2
                    sctr[0] += 1
                    bcol = pas * 16 + c
                    P.op("act", lambda e, bgt=bgt, si=si, bcol=bcol: e.activation(out=sgt[si], in_=ps[bgt][:, :], func=AF.Sigmoid,
                                                                                  bias=bg[:, bcol:bcol + 1], scale=1.0),
                         reads=[Bps[bgt], Bbg], writes=[Bsgt[si]])
                    if pas == 0:
                        P.op("dve", lambda e, by=by, si=si, c=c, tsl=tsl: e.tensor_tensor(out=mergedT[:, c, tsl], in0=ps[by][:, :], in1=sgt[si],
                                                                                         op=ALU.mult),
                             reads=[Bps[by], Bsgt[si]], writes=[BmT[c]], par=True)
                    else:
                        P.op("dve", lambda e, by=by, si=si: e.tensor_tensor(out=m1t[si], in0=ps[by][:, :], in1=sgt[si], op=ALU.mult),
                             reads=[Bps[by], Bsgt[si]], writes=[Bm1t[si]])
                        P.op("dve", lambda e, si=si, c=c, tsl=tsl: e.tensor_tensor(out=mergedT[:, c, tsl], in0=mergedT[:, c, tsl], in1=m1t[si],
                                                                                  op=ALU.add),
                             reads=[Bm1t[si], BmT[c]], writes=[BmT[c]], par=True)
    dump("mergedT", pd(64, 16 * 1024), BmT, [128, 16 * 1024])
    P.barrier()

    hreg = pd(0, 8 * 2048, F32).rearrange("p (t n) -> p t n", t=8)
    Bh = [Buf("h%d" % i) for i in range(8)]
    for t in range(8):
        P.op("sp", lambda e, t=t: e.dma_start(out=hreg[:, t, :], in_=x_own[t * 128:(t + 1) * 128, :]), dma_into=Bh[t])
    P.op("sp", lambda e: e.dma_start(out=g_bc, in_=norm_ffn_g.partition_broadcast(128)), dma_into=Bgbc)
    fT = pd(96, 16 * 1024).rearrange("p (c n) -> p c n", c=16)
    BfT = [Buf("fT%d" % i) for i in range(8)]
    xn_c = pd(128, 2048)
    Bxn_c = Buf("xnc")
    h_stats = {}

    def h_stt(t):
        rstd, brs = h_stats[t]
        P.op("dve", lambda e, t=t, rstd=rstd: e.scalar_tensor_tensor(out=xn_c, in0=hreg[:, t, :], scalar=rstd, in1=g_bc,
                                                                    op0=ALU.mult, op1=ALU.mult),
             reads=[Bh[t], brs, Bgbc], writes=[Bxn_c])
    for cg in range(4):
        Wo_, BWo_ = wtile_cols(w_out, cg * 512, 512)
        for t in range(8):
            bi = bank()
            mm_group(bi, [(ps[bi][:, :], mergedT[:, k, t * 128:(t + 1) * 128], Wo_[:, k, :]) for k in range(16)], [BWo_] + BmT)
            P.op("dve", lambda e, bi=bi, t=t, cg=cg: e.tensor_tensor(out=hreg[:, t, cg * 512:(cg + 1) * 512],
                                                                    in0=ps[bi][:, :], in1=hreg[:, t, cg * 512:(cg + 1) * 512], op=ALU.add),
                 reads=[Bps[bi], Bh[t]], writes=[Bh[t]], par=True)
            if cg == 3:
                if t < 7:
                    h_stats[t] = rms_rstd(hreg[:, t, :].rearrange("p (c n) -> p c n", c=16), [Bh[t]], D,
                                          fT[:, :, 7 * 128:8 * 128], BfT[7], par_junk=True)
                else:
                    h_stats[t] = rms_rstd(hreg[:, t, :].rearrange("p (c n) -> p c n", c=16), [Bh[t]], D,
                                          fT[:, :, 6 * 128:7 * 128], BfT[6], par_junk=True)
                if t >= 2:
                    phase1_b(t - 2, [xn_c], [Bxn_c], dst=fT, Bdst=BfT)
                if t >= 1:
                    h_stt(t - 1)
    phase1_b(6, [xn_c], [Bxn_c], dst=fT, Bdst=BfT)
    h_stt(7)
    phase1_b(7, [xn_c], [Bxn_c], dst=fT, Bdst=BfT)
    dump("h1", pd(0, 8 * 2048, F32), Bh, [128, 8 * 2048])

    actT = pd(64, 12 * 1024).rearrange("p (c n) -> p c n", c=12)
    Bact = [Buf("act%d" % i) for i in range(12)]
    sg = [pd(88, 512, F32), pd(90, 512, F32), pd(92, 512, F32), pd(94, 512, F32)]
    Bsgl = [Buf("sg%d" % i) for i in range(4)]
    gctr = [0]
    parts = [(0, 3), (3, 3), (6, 3), (9, 2)]
    outs = []
    w_down_v = w_down.rearrange("(c p) n -> p c n", p=128)
    for (cg0, ncg) in parts:
        nch = ncg * 4
        for cgi in range(ncg):
            cg = cg0 + cgi
            Wgt, BWgt = wtile_cols(w_gate, cg * 512, 512)
            Wup, BWup = wtile_cols(w_up, cg * 512, 512)
            for c4 in range(4):
                ci = cgi * 4 + c4
                for tg in range(2):
                    tsl = slice(tg * 512, (tg + 1) * 512)
                    bgt = bank()
                    mm_group(bgt, [(ps[bgt][:, :], Wgt[:, k, c4 * 128:(c4 + 1) * 128], fT[:, k, tsl]) for k in range(16)],
                             [BWgt] + BfT[4 * tg:4 * tg + 4])
                    bu = bank()
                    mm_group(bu, [(ps[bu][:, :], Wup[:, k, c4 * 128:(c4 + 1) * 128], fT[:, k, tsl]) for k in range(16)],
                             [BWup] + BfT[4 * tg:4 * tg + 4])
                    gi = gctr[0] % 4
                    gctr[0] += 1
                    P.op("act", lambda e, bgt=bgt, gi=gi: e.activation(out=sg[gi], in_=ps[bgt][:, :], func=AF.Silu),
                         reads=[Bps[bgt]], writes=[Bsgl[gi]])
                    P.op("dve", lambda e, bu=bu, gi=gi, ci=ci, tsl=tsl: e.tensor_tensor(out=actT[:, ci, tsl], in0=ps[bu][:, :], in1=sg[gi],
                                                                                       op=ALU.mult),
                         reads=[Bps[bu], Bsgl[gi]], writes=[Bact[ci]], par=True)
        last_part = (cg0 + ncg == 11)
        if last_part:
            P.op("sp", lambda e: e.dma_start(out=g_bc, in_=norm_final_g.partition_broadcast(128)), dma_into=Bgbc)
        Wds = []
        for dcg in range(4):
            src = w_down_v[:, cg0 * 4:cg0 * 4 + nch, dcg * 512:(dcg + 1) * 512]
            sld, BWd = wload([(lambda a, nch=nch: a[:, 0:nch * 512].rearrange("p (c n) -> p c n", c=nch), src)])
            Wds.append((sld[:, 0:nch * 512].rearrange("p (c n) -> p c n", c=nch), BWd))
            if not last_part:
                Wd = Wds[dcg][0]
                for t in range(8):
                    bi = bank()
                    mm_group(bi, [(ps[bi][:, :], actT[:, k, t * 128:(t + 1) * 128], Wd[:, k, :]) for k in range(nch)],
                             [BWd] + Bact[0:nch])
                    P.op("dve", lambda e, bi=bi, t=t, dcg=dcg: e.tensor_tensor(out=hreg[:, t, dcg * 512:(dcg + 1) * 512],
                                                                              in0=ps[bi][:, :], in1=hreg[:, t, dcg * 512:(dcg + 1) * 512], op=ALU.add),
                         reads=[Bps[bi], Bh[t]], writes=[Bh[t]], par=True)
        if last_part:
            junk_d = pd(96, 2048)
            Bjunk_d = Buf("junkd")
            for t in range(8):
                for dcg in range(4):
                    Wd, BWd = Wds[dcg]
                    bi = bank()
                    mm_group(bi, [(ps[bi][:, :], actT[:, k, t * 128:(t + 1) * 128], Wd[:, k, :]) for k in range(nch)],
                             [BWd] + Bact[0:nch])
                    P.op("dve", lambda e, bi=bi, t=t, dcg=dcg: e.tensor_tensor(out=hreg[:, t, dcg * 512:(dcg + 1) * 512],
                                                                              in0=ps[bi][:, :], in1=hreg[:, t, dcg * 512:(dcg + 1) * 512], op=ALU.add),
                         reads=[Bps[bi], Bh[t]], writes=[Bh[t]], par=True)
                def fin_norm(t):
                    rstd, brs = rms_rstd(hreg[:, t, :], [Bh[t]], D, junk_d, Bjunk_d)
                    P.op("dve", lambda e, t=t, rstd=rstd: e.scalar_tensor_tensor(out=hreg[:, t, :], in0=hreg[:, t, :], scalar=rstd, in1=g_bc,
                                                                                op0=ALU.mult, op1=ALU.mult),
                         reads=[Bh[t], brs, Bgbc], writes=[Bh[t]])
                    outs.append(P.op("sp", lambda e, t=t: e.dma_start(out=out_d[t * 128:(t + 1) * 128, :], in_=hreg[:, t, :]),
                                     reads=[Bh[t]], dma_out=Buf("out%d" % t)))
                if t >= 1:
                    fin_norm(t - 1)
                if t == 7:
                    fin_norm(7)
    P.run_block(final_waits=outs + list(dbg_d.values()))
    for c in reversed(ps_ctx):
        c.__exit__(None, None, None)
    arena_ctx.__exit__(None, None, None)
    return nc


def _consts(r):
    k = np.arange(128)
    ident = np.eye(128, dtype=np.float32)
    tri = np.where(k[None, :] >= k[:, None], 0.0, NEG).astype(np.float32)
    omask = np.full((128, 128), 0.0 if r == 1 else NEG, np.float32)
    mask01 = (k[None, :] >= k[:, None]).astype(np.float32)
    inv_freq = (np.float32(10000.0) ** (-(np.arange(0, 64, 2, dtype=np.float32)) / np.float32(64))).astype(np.float32)
    invf = np.tile(inv_freq, 4).reshape(128, 1).astype(np.float32)
    sgn = np.where((k % 64) < 32, -1.0, 1.0).astype(np.float32).reshape(128, 1)
    return dict(ident=ident, tri=tri, omask=omask, mask01=mask01, invf=invf, sgn=sgn)


_NC_CACHE = {}


def kernel(x, positions, norm_mix_g, w_in, b_gate, q_norm_g, w_uq, kv_norm_g, w_ukv, w_o_attn,
           sgu_norm_g, w_sgu, b_sgu, w_o_sgu, w_out, norm_ffn_g, w_gate_ffn, w_up_ffn,
           w_down_ffn, norm_final_g):
    f = lambda a: np.ascontiguousarray(np.asarray(a, dtype=np.float32))
    x = f(x)
    positions = np.ascontiguousarray(np.asarray(positions, dtype=np.int32))
    shared = dict(
        norm_mix_g=f(norm_mix_g).reshape(D), w_in=f(w_in).reshape(D, 7232), b_gate=f(b_gate).reshape(4096),
        q_norm_g=f(q_norm_g).reshape(512), w_uq=f(w_uq).reshape(512, 3072), kv_norm_g=f(kv_norm_g).reshape(512),
        w_ukv=f(w_ukv).reshape(512, 4096), w_o_attn=f(w_o_attn).reshape(D, D), sgu_norm_g=f(sgu_norm_g).reshape(1024),
        w_sgu=f(w_sgu).reshape(8, 128, 128), b_sgu=f(b_sgu).reshape(1024), w_o_sgu=f(w_o_sgu).reshape(1024, D),
        w_out=f(w_out).reshape(D, D), norm_ffn_g=f(norm_ffn_g).reshape(D), w_gate_ffn=f(w_gate_ffn).reshape(D, DFF),
        w_up_ffn=f(w_up_ffn).reshape(D, DFF), w_down_ffn=f(w_down_ffn).reshape(DFF, D), norm_final_g=f(norm_final_g).reshape(D),
    )
    in_maps = []
    for c in range(8):
        b, r = c // 2, c % 2
        xb = x[b].reshape(16, 128, D)
        pb = positions[b].reshape(16, 128)
        m = dict(shared)
        m["x_own"] = np.ascontiguousarray(xb[r::2].reshape(T, D))
        m["x_oth"] = np.ascontiguousarray(xb[1 - r::2].reshape(T, D))
        m["pos"] = np.ascontiguousarray(np.concatenate([pb[r::2].reshape(-1), pb[1 - r::2].reshape(-1)])[None, :])
        m.update(_consts(r))
        in_maps.append(m)
    if "nc" not in _NC_CACHE:
        _NC_CACHE["nc"] = build_program()
    nc = _NC_CACHE["nc"]
    res = run_bass_kernel_spmd(nc, in_maps, core_ids=list(range(8)))
    out = np.empty((4, 16, 128, D), np.float32)
    for c in range(8):
        b, r = c // 2, c % 2
        out[b, r::2] = np.asarray(res.results[c]["out"], dtype=np.float32).reshape(8, 128, D)
    kernel.last_results = res
    return out.reshape(4, S, D)
```

```python
import math
import os
import numpy as np
import concourse.bass as bass
import concourse.mybir as mybir
from concourse.bass_utils import run_bass_kernel_spmd

F32 = mybir.dt.float32
BF16 = mybir.dt.bfloat16
I32 = mybir.dt.int32
AF = mybir.ActivationFunctionType
ALU = mybir.AluOpType

D = 2048
T = 1024
S = 2048
NH = 16
DFF = 5632
EPS = 1e-6
OQ, OKV, OKPE, OU, OV, OG0, OG1 = 0, 512, 1024, 1088, 2112, 3136, 5184
SCALE = 192.0 ** -0.5
NEG = -30000.0


class Op:
    __slots__ = ("eng", "fn", "deps", "signal", "sem", "val", "group", "dma", "needed", "name")

    def resolve(self):
        if self.group is not None:
            return self.group[-1]
        return self


class Buf:
    def __init__(self, name):
        self.name = name
        self.writers = {}
        self.readers = {}
        self.dsem = None
        self.dcount = 0


class Prog:
    ENGS = ("pe", "act", "dve", "pool", "sp")

    def __init__(self, nc):
        self.nc = nc
        self.ops = {e: [] for e in self.ENGS}
        self.sems = {}
        self.nops = 0
        self._sem_ctx = []

    def new_sem(self, name):
        ctx = self.nc.semaphore(name)
        s = ctx.__enter__()
        self._sem_ctx.append(ctx)
        return s

    def close(self):
        for c in reversed(self._sem_ctx):
            c.__exit__(None, None, None)

    def op(self, eng, fn, reads=(), writes=(), dma_into=None, dma_out=None, group=None,
           par=False, name=None, extra_deps=()):
        o = Op()
        o.eng = eng
        o.fn = fn
        o.signal = False
        o.needed = False
        o.group = group
        o.dma = (dma_into is not None) or (dma_out is not None)
        o.name = name
        o.sem = None
        o.val = None
        self.nops += 1
        deps = []
        seen = set()

        def add(d):
            if d is not None and id(d) not in seen and d is not o:
                seen.add(id(d))
                deps.append(d)

        wr = list(writes)
        if dma_into is not None:
            wr.append(dma_into)
        for b in reads:
            for d in b.writers.values():
                add(d)
        for b in wr:
            for d in b.readers.values():
                add(d)
            if not par:
                for d in b.writers.values():
                    add(d)
        for d in extra_deps:
            add(d)
        o.deps = deps
        for d in deps:
            d.needed = True
        key = ("dma", self.nops) if o.dma else eng
        for b in reads:
            b.readers[key] = o
        for b in wr:
            if par:
                b.writers[key] = o
            else:
                b.writers = {key: o}
                b.readers = {}
        sb = dma_into if dma_into is not None else dma_out
        if sb is not None:
            if sb.dsem is None:
                sb.dsem = self.new_sem("d_" + sb.name)
            sb.dcount += 1
            o.sem = sb.dsem
            o.val = 16 * sb.dcount
            o.signal = True
        if group is not None:
            group.append(o)
        self.ops[eng].append(o)
        return o

    def last(self, eng):
        for o in reversed(self.ops[eng]):
            if o.fn is not None and not o.dma:
                return o
        return None

    def barrier(self):
        lasts = {e: self.last(e) for e in ("pe", "act", "dve")}
        for e in ("pe", "act", "dve", "sp"):
            self.op(e, None, extra_deps=[lasts[x] for x in lasts])

    def finalize(self):
        for e in self.ENGS:
            for o in self.ops[e]:
                if o.needed and not o.dma:
                    o.resolve().signal = True
        for e in self.ENGS:
            sem = self.new_sem("e_" + e)
            self.sems[e] = sem
            cnt = 0
            for o in self.ops[e]:
                if o.dma:
                    continue
                if o.signal:
                    cnt += 1
                    o.sem = sem
                    o.val = cnt

    def emit(self, eng_name, eng):
        waited = {}
        for o in self.ops[eng_name]:
            need = {}
            for d in o.deps:
                r = d.resolve()
                if (not r.dma) and r.eng == "pe" and eng_name == "pe":
                    continue
                k = id(r.sem)
                if k not in need or need[k][1] < r.val:
                    need[k] = (r.sem, r.val)
            for k, (sem, val) in need.items():
                if waited.get(k, 0) < val:
                    eng.wait_ge(sem, val)
                    waited[k] = val
            if o.fn is None:
                continue
            ins = o.fn(eng)
            if o.signal:
                ins.then_inc(o.sem, 16 if o.dma else 1)

    def run_block(self, final_waits=()):
        self.finalize()
        nc = self.nc
        with nc.Block() as block:
            @block.tensor
            def _(e):
                self.emit("pe", e)

            @block.scalar
            def _(e):
                self.emit("act", e)

            @block.vector
            def _(e):
                self.emit("dve", e)

            @block.gpsimd
            def _(e):
                self.emit("pool", e)

            @block.sync
            def _(e):
                self.emit("sp", e)
                for o in final_waits:
                    e.wait_ge(o.sem, o.val)
        self.close()


WR = 0
CONST = 65536
PD = 75776
KB = 1024
TOTAL = PD + 133 * KB

DEBUG = os.environ.get("MK_DEBUG", "")


def build_program():
    nc = bass.Bass("TRN2", target_bir_lowering=False)

    def din(name, shape, dt=F32):
        return nc.dram_tensor(name, list(shape), dt, kind="ExternalInput").ap()

    x_own = din("x_own", [T, D])
    x_oth = din("x_oth", [T, D])
    pos_d = din("pos", [1, S], I32)
    ident_d = din("ident", [128, 128])
    tri_d = din("tri", [128, 128])
    omask_d = din("omask", [128, 128])
    mask01_d = din("mask01", [128, 128])
    invf_d = din("invf", [128, 1])
    sgn_d = din("sgn", [128, 1])
    norm_mix_g = din("norm_mix_g", [D])
    w_in = din("w_in", [D, 7232])
    b_gate = din("b_gate", [4096])
    q_norm_g = din("q_norm_g", [512])
    w_uq = din("w_uq", [512, 3072])
    kv_norm_g = din("kv_norm_g", [512])
    w_ukv = din("w_ukv", [512, 4096])
    w_o_attn = din("w_o_attn", [D, D])
    sgu_norm_g = din("sgu_norm_g", [1024])
    w_sgu = din("w_sgu", [8, 128, 128])
    b_sgu = din("b_sgu", [1024])
    w_o_sgu = din("w_o_sgu", [1024, D])
    w_out = din("w_out", [D, D])
    norm_ffn_g = din("norm_ffn_g", [D])
    w_gate = din("w_gate_ffn", [D, DFF])
    w_up = din("w_up_ffn", [D, DFF])
    w_down = din("w_down_ffn", [DFF, D])
    norm_final_g = din("norm_final_g", [D])
    out_d = nc.dram_tensor("out", [T, D], F32, kind="ExternalOutput").ap()
    dbg_d = {}

    P = Prog(nc)
    arena_ctx = nc.sbuf_tensor("arena", [128, TOTAL // 2], BF16)
    arena = arena_ctx.__enter__()
    ps_ctx = [nc.psum_tensor("ps%d" % i, [128, 512], F32) for i in range(8)]
    ps = [c.__enter__() for c in ps_ctx]
    Bps = [Buf("ps%d" % i) for i in range(8)]

    def view(off, cols, dt=BF16):
        esz = 2 if dt == BF16 else 4
        nb = cols * esz
        assert off % 4 == 0 and off + nb <= TOTAL, (off, nb)
        v = arena[:, off // 2: (off + nb) // 2]
        if dt != BF16:
            v = v.bitcast(dt)
        return v

    def pd(kb_off, cols, dt=BF16):
        return view(PD + int(kb_off * KB), cols, dt)

    g_bc = view(CONST, 2048, F32)
    ident = view(CONST + 8192, 128)
    tri = view(CONST + 8448, 128)
    omask = view(CONST + 8704, 128)
    bg = view(CONST + 8960, 32, F32)
    invf = view(CONST + 9088, 1, F32)
    sgn = view(CONST + 9120, 1, F32)
    stats = view(CONST + 9152, 64, F32)
    Bgbc, Bcst, Bsmall = Buf("gbc"), Buf("cst"), Buf("small")
    Bst = [Buf("st%d" % i) for i in range(16)]
    st_ctr = [0]

    def stat():
        i = st_ctr[0] % 16
        st_ctr[0] += 1
        return stats[:, 4 * i: 4 * i + 1], stats[:, 4 * i + 1: 4 * i + 2], Bst[i]

    NSLOT = 4
    Bw = [Buf("w%d" % i) for i in range(NSLOT)]
    wslot = [view(WR + i * 16384, 8192) for i in range(NSLOT)]
    wctr = [0]

    def wload(pieces, extra=()):
        s = wctr[0] % NSLOT
        wctr[0] += 1
        for i, (dfn, src) in enumerate(pieces):
            P.op("pool", lambda e, d=dfn(wslot[s]), sr=src: e.dma_start(out=d, in_=sr),
                 dma_into=Bw[s], par=(i > 0), extra_deps=(extra if i == 0 else ()))
        return wslot[s], Bw[s]

    def wtile_cols(w_ap, c0, ncols, kc=16, extra=()):
        src = w_ap.rearrange("(c p) n -> p c n", p=128)[:, :, c0:c0 + ncols]
        sl, b = wload([(lambda a: a[:, 0:kc * ncols].rearrange("p (c n) -> p c n", c=kc), src)], extra=extra)
        return sl[:, 0:kc * ncols].rearrange("p (c n) -> p c n", c=kc), b

    bank_ctr = [0]
    bank_set = [list(range(8))]

    def bank():
        lst = bank_set[0]
        i = lst[bank_ctr[0] % len(lst)]
        bank_ctr[0] += 1
        return i

    def mm_group(bi, items, reads):
        grp = []
        n = len(items)
        for i, (o_ap, l, r) in enumerate(items):
            P.op("pe", lambda e, o_ap=o_ap, l=l, r=r, i=i: e.matmul(o_ap, l, r, start=(i == 0), stop=(i == n - 1)),
                 reads=reads, writes=[Bps[bi]], group=grp)
        return grp

    def transposes(bi, srcs, reads):
        grp = []
        for i, sap in enumerate(srcs):
            P.op("pe", lambda e, sap=sap, i=i: e.matmul(ps[bi][:, i * 128:(i + 1) * 128], sap, ident, start=True, stop=True),
                 reads=reads + [Bcst], writes=[Bps[bi]], group=grp)
        return grp

    ev_ctr = [0]

    def copy_evac(out_ap, in_ap, reads, writes, par=True, eng=None):
        if eng is None:
            eng = ("act", "dve")[ev_ctr[0] % 2]
            ev_ctr[0] += 1
        if eng == "act":
            return P.op("act", lambda e: e.copy(out=out_ap, in_=in_ap), reads=reads, writes=writes, par=par)
        return P.op("dve", lambda e: e.tensor_copy(out=out_ap, in_=in_ap), reads=reads, writes=writes, par=par)

    def rms_rstd(src_ap, src_bufs, n, junk_ap, Bjunk, par_junk=False):
        a0, a1, b = stat()
        P.op("act", lambda e: e.activation(out=junk_ap, in_=src_ap, func=AF.Square, scale=float(n) ** -0.5, accum_out=a0),
             reads=src_bufs, writes=[b])
        P.ops["act"][-1].deps.extend(d for d in (list(Bjunk.readers.values()) + list(Bjunk.writers.values()))
                                     if d not in P.ops["act"][-1].deps and d is not P.ops["act"][-1])
        for d in P.ops["act"][-1].deps:
            d.needed = True
        if par_junk:
            Bjunk.writers["act"] = P.ops["act"][-1]
        else:
            Bjunk.writers = {"act": P.ops["act"][-1]}
            Bjunk.readers = {}
        P.op("act", lambda e: e.activation(out=a1, in_=a0, func=AF.Sqrt, bias=epsb, scale=1.0), reads=[b, Bsmall], writes=[b])
        P.op("dve", lambda e: e.reciprocal(out=a1, in_=a1), reads=[b], writes=[b])
        return a1, b

    def dump(name, ap, bufs, shape):
        if DEBUG and name in DEBUG.split(","):
            dt = ap.dtype
            dd = nc.dram_tensor("dbg_" + name, list(shape), dt, kind="ExternalOutput").ap()
            dbg_d[name] = P.op("sp", lambda e: e.dma_start(out=dd, in_=ap), reads=bufs, dma_out=Buf("dbg_" + name))

    epsb = view(CONST + 9152 + 256, 1, F32)
    P.op("pool", lambda e: e.dma_start(out=ident, in_=ident_d), dma_into=Bcst)
    P.op("pool", lambda e: e.dma_start(out=tri, in_=tri_d), dma_into=Bcst, par=True)
    P.op("pool", lambda e: e.dma_start(out=omask, in_=omask_d), dma_into=Bcst, par=True)
    gbc_loaded = [False]
    P.op("sp", lambda e: e.dma_start(out=invf, in_=invf_d), dma_into=Bsmall)
    P.op("sp", lambda e: e.dma_start(out=sgn, in_=sgn_d), dma_into=Bsmall, par=True)
    P.op("dve", lambda e: e.memset(epsb, EPS), writes=[Bsmall], par=True)

    CC = pd(32, 2048, F32)
    SS = pd(40, 2048, F32)
    Btab = Buf("tab")
    TWO_PI = 2.0 * math.pi
    Btt = Buf("tabtmp")

    def table_gen():
        for hf in range(2):
            tsl = slice(hf * 1024, (hf + 1) * 1024)
            posi = pd(88, 1024, I32)
            posf = pd(92, 1024, F32)
            ang = pd(96, 1024, F32)
            kf = pd(100, 1024, F32)
            ki = pd(104, 1024, I32)
            Bt = Btt
            P.op("sp", lambda e, posi=posi, tsl=tsl: e.dma_start(out=posi, in_=pos_d[:, tsl].partition_broadcast(128)),
                 dma_into=Bt)
            P.op("dve", lambda e, posi=posi, posf=posf: e.tensor_copy(out=posf, in_=posi), reads=[Bt], writes=[Bt])
            yield
            for which in range(2):
                P.op("dve", lambda e, which=which: e.tensor_scalar(out=ang, in0=posf, scalar1=invf[:, 0:1],
                                                                     scalar2=(0.5 * math.pi if which else 0.0),
                                                                     op0=ALU.mult, op1=ALU.add),
                     reads=[Bt, Bsmall], writes=[Bt])
                yield
                P.op("dve", lambda e: e.tensor_scalar(out=kf, in0=ang, scalar1=1.0 / TWO_PI, scalar2=None, op0=ALU.mult),
                     reads=[Bt], writes=[Bt])
                yield
                P.op("dve", lambda e: e.tensor_copy(out=ki, in_=kf), reads=[Bt], writes=[Bt])
                yield
                P.op("dve", lambda e: e.tensor_copy(out=kf, in_=ki), reads=[Bt], writes=[Bt])
                yield
                P.op("dve", lambda e: e.scalar_tensor_tensor(out=ang, in0=kf, scalar=-TWO_PI, in1=ang, op0=ALU.mult, op1=ALU.add),
                     reads=[Bt], writes=[Bt])
                yield
                P.op("dve", lambda e: e.tensor_scalar(out=kf, in0=ang, scalar1=math.pi, scalar2=-TWO_PI, op0=ALU.is_gt, op1=ALU.mult),
                     reads=[Bt], writes=[Bt])
                yield
                P.op("dve", lambda e: e.tensor_tensor(out=ang, in0=ang, in1=kf, op=ALU.add), reads=[Bt], writes=[Bt])
                yield
                dst = (CC if which else SS)[:, tsl]
                P.op("act", lambda e, dst=dst: e.activation(out=dst, in_=ang, func=AF.Sin), reads=[Bt], writes=[Btab], par=True)
                if not which:
                    P.op("dve", lambda e, dst=dst: e.tensor_scalar(out=dst, in0=dst, scalar1=sgn[:, 0:1], scalar2=None, op0=ALU.mult),
                         reads=[Btab, Bsmall], writes=[Btab])
                yield

    tgen = [table_gen()]

    def tick(n=2):
        for _ in range(n):
            if tgen[0] is None:
                return
            try:
                next(tgen[0])
            except StopIteration:
                tgen[0] = None


    aT = pd(0, 16 * 1024).rearrange("p (c n) -> p c n", c=16)
    BaT = [Buf("aT%d" % i) for i in range(8)]

    x_dma_ops = []

    def phase1_a(t, x_d, xs_list, xns, Bxs, Bxns, gbuf=Bgbc, hsrc=None, junk=None):
        if hsrc is None:
            xs = xs_list[t % 2]
            bx = Bxs[t % 2]
            x_dma_ops.append(P.op("sp", lambda e, xs=xs, t=t: e.dma_start(out=xs, in_=x_d[t * 128:(t + 1) * 128, :]), dma_into=bx))
            if not gbc_loaded[0]:
                gbc_loaded[0] = True
                P.op("sp", lambda e: e.dma_start(out=g_bc, in_=norm_mix_g.partition_broadcast(128)), dma_into=Bgbc)
        else:
            xs, bx = hsrc[t]
        xn, Bxn = xns[t % len(xns)], Bxns[t % len(xns)]
        if junk is None:
            rstd, brs = rms_rstd(xs, [bx], D, xn, Bxn)
        else:
            rstd, brs = rms_rstd(xs.rearrange("p (c n) -> p c n", c=16), [bx], D, junk[0], junk[1], par_junk=True)
        P.op("dve", lambda e, xs=xs, rstd=rstd, xn=xn: e.scalar_tensor_tensor(out=xn, in0=xs, scalar=rstd, in1=g_bc,
                                                                             op0=ALU.mult, op1=ALU.mult),
             reads=[bx, brs, gbuf], writes=[Bxn])

    def phase1_b(t, xns, Bxns, dst=aT, Bdst=BaT):
        xn, Bxn = xns[t % len(xns)], Bxns[t % len(xns)]
        for cg in range(4):
            bi = bank()
            transposes(bi, [xn[:, (4 * cg + i) * 128:(4 * cg + i + 1) * 128] for i in range(4)], [Bxn])
            copy_evac(dst[:, 4 * cg:4 * cg + 4, t * 128:(t + 1) * 128],
                      ps[bi][:, :].rearrange("p (c n) -> p c n", c=4), [Bps[bi]], [Bdst[t]],
                      eng=("act" if (tgen[0] is not None or cg < 3) else "dve"))

    def phase1_tile(t, x_d, xs_list, xns, Bxs, Bxns, gbuf=Bgbc, dst=aT, Bdst=BaT, hsrc=None):
        phase1_a(t, x_d, xs_list, xns, Bxs, Bxns, gbuf, hsrc)
        phase1_b(t, xns, Bxns, dst, Bdst)

    def phase1(x_d, xs_list, xns, Bxs, Bxns, gbuf=Bgbc, dst=aT, Bdst=BaT, hsrc=None, ticks=0):
        assert len(xns) >= 2
        for t in range(8):
            phase1_a(t, x_d, xs_list, xns, Bxs, Bxns, gbuf, hsrc)
            if t >= 1:
                phase1_b(t - 1, xns, Bxns, dst, Bdst)
            tick(ticks)
        phase1_b(7, xns, Bxns, dst, Bdst)

    kvnT = pd(64, 4 * 2048).rearrange("p (c n) -> p c n", c=4)
    qnT = pd(80, 4 * 1024).rearrange("p (c n) -> p c n", c=4)
    kpeT = pd(88, 2048)
    qpeT = pd(92, 8 * 1024).rearrange("p (h n) -> p h n", h=8)
    BkvnT = [Buf("kvnT%d" % i) for i in range(16)]
    BqnT = [Buf("qnT%d" % i) for i in range(8)]
    BkpeT = [Buf("kpeT%d" % i) for i in range(4)]
    BqpeT = [Buf("qpeT%d" % i) for i in range(8)]

    xs_a = [pd(48, 2048, F32), pd(56, 2048, F32)]
    Bxs_a = [Buf("xs0"), Buf("xs1")]
    xn_a = [pd(108, 2048), pd(112, 2048)]
    Bxn_a = [Buf("xn0"), Buf("xn1")]
    junk_a = pd(130, 512)
    Bjunk_a = Buf("junk")
    gkv_bc = pd(116, 512, F32)
    gq_bc = pd(118, 512, F32)
    lat = [pd(120, 512), pd(121, 512)]
    Blat = [Buf("lat0"), Buf("lat1")]
    rt = [pd(122, 512, F32), pd(124, 512, F32)]
    Brt = [Buf("rt%d" % i) for i in range(2)]
    kpeB = pd(126, 2048)
    Bgsm = Buf("gsm")
    P.op("sp", lambda e: e.dma_start(out=gkv_bc, in_=kv_norm_g.partition_broadcast(128)), dma_into=Bgsm)
    P.op("sp", lambda e: e.dma_start(out=gq_bc, in_=q_norm_g.partition_broadcast(128)), dma_into=Bgsm, par=True)

    lat_ctr = [0]

    def latent_a(t, W, Bwt, gs_bc):
        bi = bank()
        mm_group(bi, [(ps[bi][:, :], aT[:, c, t * 128:(t + 1) * 128], W[:, c, :]) for c in range(16)], [BaT[t], Bwt])
        rstd, brs = rms_rstd(ps[bi][:, :], [Bps[bi]], 512, junk_a, Bjunk_a)
        li = lat_ctr[0] % 2
        lat_ctr[0] += 1
        P.op("dve", lambda e: e.scalar_tensor_tensor(out=lat[li], in0=ps[bi][:, :], scalar=rstd, in1=gs_bc,
                                                    op0=ALU.mult, op1=ALU.mult),
             reads=[Bps[bi], brs, Bgsm], writes=[Blat[li]])
        return li

    def latent_b(li, outT, Bout, tok0):
        b2 = bank()
        transposes(b2, [lat[li][:, i * 128:(i + 1) * 128] for i in range(4)], [Blat[li]])
        copy_evac(outT[:, :, tok0:tok0 + 128], ps[b2][:, :].rearrange("p (c n) -> p c n", c=4), [Bps[b2]], [Bout])

    def latents(W, Bwt, gs_bc, outT, Bouts, tok_base):
        prev = None
        for t in range(8):
            li = latent_a(t, W, Bwt, gs_bc)
            tick(2)
            if prev is not None:
                latent_b(prev[0], outT, Bouts[prev[1]], tok_base + prev[1] * 128)
            prev = (li, t)
        latent_b(prev[0], outT, Bouts[prev[1]], tok_base + prev[1] * 128)

    rope_ctr = [0]

    def rope_proj(items_n, items_s, reads, tab0, outs, Bout):
        while tgen[0] is not None:
            tick(1)
        ba = bank()
        mm_group(ba, [(ps[ba][:, :], l, r) for (l, r) in items_n], reads)
        bb = bank()
        mm_group(bb, [(ps[bb][:, :], l, r) for (l, r) in items_s], reads)
        t1, t2 = rt[0], rt[1]
        P.op("dve", lambda e: e.tensor_tensor(out=t1, in0=ps[ba][:, :], in1=CC[:, tab0:tab0 + 512], op=ALU.mult),
             reads=[Bps[ba], Btab], writes=[Brt[0]])
        P.op("dve", lambda e: e.tensor_tensor(out=t2, in0=ps[bb][:, :], in1=SS[:, tab0:tab0 + 512], op=ALU.mult),
             reads=[Bps[bb], Btab], writes=[Brt[1]])
        for (psl, o_ap) in outs:
            P.op("dve", lambda e, psl=psl, o_ap=o_ap: e.tensor_tensor(out=o_ap, in0=t1[psl, :], in1=t2[psl, :], op=ALU.add),
                 reads=[Brt[0], Brt[1]], writes=[Bout], par=True)

    w_in_v = w_in.rearrange("(c p) n -> p c n", p=128)

    def kpe_dst(a, o, n):
        return a[:, 0:16 * 256].rearrange("p (c n) -> p c n", c=16)[:, :, o:o + n]
    WP = {}

    def load_p2a_weights():
        if WP:
            return
        ex = [x_dma_ops[min(5, len(x_dma_ops) - 1)]]
        WP["kv"] = wtile_cols(w_in, OKV, 512, extra=ex)
        slk, BWkpe_ = wload([
            (lambda a: kpe_dst(a, 0, 64), w_in_v[:, :, OKPE:OKPE + 64]),
            (lambda a: kpe_dst(a, 64, 64), w_in_v[:, :, OKPE:OKPE + 64]),
            (lambda a: kpe_dst(a, 128, 32), w_in_v[:, :, OKPE + 32:OKPE + 64]),
            (lambda a: kpe_dst(a, 160, 32), w_in_v[:, :, OKPE:OKPE + 32]),
            (lambda a: kpe_dst(a, 192, 32), w_in_v[:, :, OKPE + 32:OKPE + 64]),
            (lambda a: kpe_dst(a, 224, 32), w_in_v[:, :, OKPE:OKPE + 32]),
        ])
        WP["kpe"] = (slk[:, 0:16 * 256].rearrange("p (c n) -> p c n", c=16), BWkpe_)

    Bkz = Buf("kz")
    P.op("dve", lambda e: e.memset(kpeB[0:64, :], 0.0), writes=[Bkz])

    def do_tokens(x_d, tok_base, with_q):
        phase1(x_d, xs_a, xn_a, Bxs_a, Bxn_a, ticks=2)
        load_p2a_weights()
        Wkv, BWkv = WP["kv"]
        Wkpe, BWkpe = WP["kpe"]
        latents(Wkv, BWkv, gkv_bc, kvnT, BkvnT[tok_base // 128:tok_base // 128 + 8], tok_base)
        for tg in range(2):
            tok0 = tok_base + tg * 512
            rope_proj([(Wkpe[:, c, 0:128], aT[:, c, tg * 512:(tg + 1) * 512]) for c in range(16)],
                      [(Wkpe[:, c, 128:256], aT[:, c, tg * 512:(tg + 1) * 512]) for c in range(16)],
                      [BWkpe] + BaT[4 * tg:4 * tg + 4], tok0,
                      [(slice(0, 64), kpeT[0:64, tok0:tok0 + 512]), (slice(64, 128), kpeB[64:128, tok0:tok0 + 512])],
                      BkpeT[tok0 // 512])

    do_tokens(x_oth, 1024, False)
    P.op("dve", lambda e: e.memset(kpeT[64:128, :], 0.0), reads=[Btab], writes=[Bkz], par=True)
    do_tokens(x_own, 0, True)
    Wq, BWq = wtile_cols(w_in, OQ, 512)
    latents(Wq, BWq, gq_bc, qnT, BqnT, 0)
    w_uq_v = w_uq.rearrange("(c p) (h e) -> p c h e", p=128, e=192)

    def qd(a, base, eo, n):
        v = a[:, 0:4 * 2048].rearrange("p (c s h e) -> p c s h e", c=4, s=2, h=16)
        return v[:, :, base, :, eo:eo + n]
    qpieces = []
    for c_ in range(4):
        qpieces += [
            (lambda a, c_=c_: qd(a, 0, 0, 64)[:, c_], w_uq_v[:, c_, :, 128:192]),
            (lambda a, c_=c_: qd(a, 1, 0, 32)[:, c_], w_uq_v[:, c_, :, 160:192]),
            (lambda a, c_=c_: qd(a, 1, 32, 32)[:, c_], w_uq_v[:, c_, :, 128:160]),
        ]
    slq, BWqpe = wload(qpieces)
    Wqpe = slq[:, 0:4 * 2048].rearrange("p (c s n) -> p c s n", c=4, s=2)
    for hp in range(8):
        for tg in range(2):
            rope_proj([(Wqpe[:, c, 0, hp * 128:(hp + 1) * 128], qnT[:, c, tg * 512:(tg + 1) * 512]) for c in range(4)],
                      [(Wqpe[:, c, 1, hp * 128:(hp + 1) * 128], qnT[:, c, tg * 512:(tg + 1) * 512]) for c in range(4)],
                      [BWqpe] + BqnT[4 * tg:4 * tg + 4], tg * 512,
                      [(slice(0, 128), qpeT[:, hp, tg * 512:(tg + 1) * 512])], BqpeT[hp])
    dump("kvnT", pd(64, 4 * 2048), BkvnT, [128, 4 * 2048])
    dump("qnT", pd(80, 4 * 1024), BqnT, [128, 4 * 1024])
    dump("kpeT", kpeT, BkpeT, [128, 2048])
    dump("qpeT", pd(92, 8 * 1024), BqpeT, [128, 8 * 1024])
    P.barrier()

    attnT = pd(32, 16 * 1024).rearrange("p (h n) -> p h n", h=16)
    BattnT = [Buf("attnT%d" % i) for i in range(16)]
    knT = pd(0, 4 * 2048).rearrange("p (h n) -> p h n", h=4)
    qnopeT = pd(16, 4 * 1024).rearrange("p (h n) -> p h n", h=4)
    v1 = pd(108, 16 * 4 * 129).rearrange("p (t h d) -> p t h d", t=16, h=4)
    BknT = [Buf("knT%d" % i) for i in range(4)]
    Bqno = [Buf("qno%d" % i) for i in range(4)]
    Bv1 = Buf("v1")
    pT = [pd(24, 512), pd(25, 512), pd(26, 512), pd(27, 512)]
    BpT = [Buf("pT%d" % i) for i in range(4)]
    atok = [pd(28, 128), pd(28.25, 128), pd(28.5, 128), pd(28.75, 128)]
    Batok = [Buf("atok%d" % i) for i in range(4)]
    rsb = pd(29, 8, F32)
    sctr3 = [0]
    Brs = [Buf("rs%d" % i) for i in range(8)]
    P.op("dve", lambda e: e.memset(pd(108, 16 * 4 * 129), 1.0), writes=[Bv1])
    pctr = [0]
    actr = [0]
    w_ukv_v = w_ukv.rearrange("(c p) n -> p c n", p=128)
    w_uq_v2 = w_uq.rearrange("(c p) n -> p c n", p=128)
    for G in range(4):
        slg, BWg = wload([
            (lambda a: a[:, 0:4096].rearrange("p (c n) -> p c n", c=4), w_ukv_v[:, :, G * 1024:(G + 1) * 1024]),
            (lambda a: a[:, 4096:4096 + 4 * 768].rearrange("p (c n) -> p c n", c=4), w_uq_v2[:, :, G * 768:(G + 1) * 768]),
        ])
        Wkv4 = slg[:, 0:4096].rearrange("p (c h t d) -> p c h t d", c=4, h=4, t=2)
        Wq4 = slg[:, 4096:4096 + 4 * 768].rearrange("p (c h e) -> p c h e", c=4, h=4)
        bank_set[0] = [0, 1, 2, 3]
        for hl in range(4):
            for tg in range(4):
                bi = bank()
                mm_group(bi, [(ps[bi][:, :], Wkv4[:, c, hl, 0, :], kvnT[:, c, tg * 512:(tg + 1) * 512]) for c in range(4)],
                         [BWg] + BkvnT[4 * tg:4 * tg + 4])
                copy_evac(knT[:, hl, tg * 512:(tg + 1) * 512], ps[bi][:, :], [Bps[bi]], [BknT[hl]])
            for tg in range(2):
                bi = bank()
                mm_group(bi, [(ps[bi][:, :], Wq4[:, c, hl, 0:128], qnT[:, c, tg * 512:(tg + 1) * 512]) for c in range(4)],
                         [BWg] + BqnT[4 * tg:4 * tg + 4])
                copy_evac(qnopeT[:, hl, tg * 512:(tg + 1) * 512], ps[bi][:, :], [Bps[bi]], [Bqno[hl]])
        for kt in range(16):
            bi = bank()
            mm_group(bi, [(ps[bi][:, :].rearrange("p (h d) -> p h d", h=4), kvnT[:, c, kt * 128:(kt + 1) * 128], Wkv4[:, c, :, 1, :])
                          for c in range(4)], [BWg, BkvnT[kt]])
            copy_evac(v1[:, kt, :, 0:128], ps[bi][:, :].rearrange("p (h d) -> p h d", h=4), [Bps[bi]], [Bv1])
        steps = []
        for hl in range(4):
            for qg in range(2):
                j0 = 4 * qg
                for i in range(j0 + 4):
                    for typ in range(2):
                        steps.append((hl, qg, i, typ))
        LAG = 2
        info = {}
        pend_tr = []

        def flush_tr(now, age=2):
            while pend_tr and (now is None or now - pend_tr[0][3] >= age):
                ai_, h_, j_, _ = pend_tr.pop(0)
                transposes(3, [atok[ai_]], [Batok[ai_]])
                copy_evac(attnT[:, h_, j_ * 128:(j_ + 1) * 128], ps[3][:, 0:128], [Bps[3]], [BattnT[h_]], eng="dve")

        for idx in range(len(steps) + LAG):
            if idx < len(steps):
                hl, qg, i, typ = steps[idx]
                h = 4 * G + hl
                hp, half = h // 2, (h % 2) * 64
                j0 = 4 * qg
                kt = i + 8 * typ
                js = max(i, j0)
                nq = (j0 + 4 - js) * 128
                q0 = js * 128
                bi = sctr3[0] % 3
                sctr3[0] += 1
                items = [(ps[bi][:, 0:nq], knT[:, hl, kt * 128:(kt + 1) * 128], qnopeT[:, hl, q0:q0 + nq]),
                         (ps[bi][:, 0:nq], (kpeB if (h % 2) else kpeT)[:, kt * 128:(kt + 1) * 128], qpeT[:, hp, q0:q0 + nq])]
                if i >= j0:
                    items.append((ps[bi][:, 0:128], ident, omask if typ else tri))
                mm_group(bi, items, [BknT[hl], Bqno[hl], BkpeT[kt // 4], BqpeT[hp], Bcst, Bkz])
                pi = pctr[0] % 4
                pctr[0] += 1
                P.op("act", lambda e, bi=bi, pi=pi, nq=nq: e.activation(out=pT[pi][:, 0:nq], in_=ps[bi][:, 0:nq],
                                                                        func=AF.Exp, scale=SCALE),
                     reads=[Bps[bi]], writes=[BpT[pi]])
                info[idx] = pi
            if idx >= LAG:
                flush_tr(idx)
                hl, qg, i, typ = steps[idx - LAG]
                pi = info.pop(idx - LAG)
                h = 4 * G + hl
                j0 = 4 * qg
                kt = i + 8 * typ
                js = max(i, j0)
                for j in range(js, j0 + 4):
                    ab = 4 + (j - j0)
                    first = (i == 0 and typ == 0)
                    lastk = (i == j and typ == 1)
                    P.op("pe", lambda e, ab=ab, pi=pi, j=j, js=js, kt=kt, hl=hl, first=first, lastk=lastk:
                         e.matmul(ps[ab][:, 0:129], pT[pi][:, (j - js) * 128:(j - js + 1) * 128], v1[:, kt, hl, :],
                                  start=first, stop=lastk),
                         reads=[BpT[pi], Bv1], writes=[Bps[ab]])
                    if lastk:
                        ri = j
                        P.op("dve", lambda e, ab=ab, ri=ri: e.reciprocal(out=rsb[:, ri:ri + 1], in_=ps[ab][:, 128:129]),
                             reads=[Bps[ab]], writes=[Brs[ri]])
                        ai = actr[0] % 4
                        actr[0] += 1
                        P.op("dve", lambda e, ab=ab, ri=ri, ai=ai: e.tensor_scalar(out=atok[ai], in0=ps[ab][:, 0:128],
                                                                                  scalar1=rsb[:, ri:ri + 1], scalar2=None,
                                                                                  op0=ALU.mult),
                             reads=[Bps[ab], Brs[ri]], writes=[Batok[ai]])
                        pend_tr.append((ai, h, j, idx))
        flush_tr(None)
    bank_set[0] = list(range(8))
    dump("attnT", pd(32, 16 * 1024), BattnT, [128, 16 * 1024])
    P.barrier()

    xs_b = [pd(64, 2048, F32), pd(72, 2048, F32)]
    Bxs_b = [Buf("xsb0"), Buf("xsb1")]
    xn_b = [pd(80, 2048), pd(84, 2048)]
    Bxn_b = [Buf("xnb0"), Buf("xnb1")]
    for t in range(4):
        phase1_a(t, x_own, xs_b, xn_b, Bxs_b, Bxn_b)
        if t >= 1:
            phase1_b(t - 1, xn_b, Bxn_b)
    sguT = pd(96, 8 * 1024).rearrange("p (g n) -> p g n", g=8)
    BsguT = [Buf("sguT%d" % i) for i in range(8)]
    gsgu_bc = pd(112, 1024, F32)
    bsgu_bc = pd(116, 1024, F32)
    wsT = pd(120, 8 * 128).rearrange("p (g n) -> p g n", g=8)
    mask01 = pd(122, 128, F32)
    wraw = pd(122.5, 8 * 128, F32).rearrange("p (g n) -> p g n", g=8)
    wrawb = pd(126.5, 8 * 128).rearrange("p (g n) -> p g n", g=8)
    tmpm = [pd(128.5, 512, F32)]
    Btmpm = [Buf("tmpm0")]
    vg = [pd(88, 1024, F32), pd(92, 1024, F32)]
    vn = [pd(130.5, 1024)]
    Bvg = [Buf("vg0"), Buf("vg1")]
    Bvn = [Buf("vn0")]
    Bsg = Buf("sgc")
    BwsT, Bwraw, Bwrawb = Buf("wsT"), Buf("wraw"), Buf("wrawb")
    Bbg = Buf("bgate")
    P.op("sp", lambda e: e.dma_start(out=bg, in_=b_gate.rearrange("(c p) -> p c", p=128), allow_slow_non_contiguous=True),
         dma_into=Bbg)
    P.op("sp", lambda e: e.dma_start(out=gsgu_bc, in_=sgu_norm_g.partition_broadcast(128)), dma_into=Bsg)
    P.op("sp", lambda e: e.dma_start(out=bsgu_bc, in_=b_sgu.partition_broadcast(128)), dma_into=Bsg, par=True)
    P.op("sp", lambda e: e.dma_start(out=mask01, in_=mask01_d), dma_into=Bsg, par=True)
    Wus = [wtile_cols(w_in, OU + half * 512, 512) for half in range(2)]

    def u_proj(g, tg):
        Wu, BWu = Wus[g // 4]
        gl = g % 4
        bi = bank()
        mm_group(bi, [(ps[bi][:, :], Wu[:, c, gl * 128:(gl + 1) * 128], aT[:, c, tg * 512:(tg + 1) * 512]) for c in range(16)],
                 [BWu] + BaT[4 * tg:4 * tg + 4])
        P.op("act", lambda e, bi=bi, g=g, tg=tg: e.activation(out=sguT[:, g, tg * 512:(tg + 1) * 512], in_=ps[bi][:, :],
                                                              func=AF.Gelu_apprx_tanh),
             reads=[Bps[bi]], writes=[BsguT[g]], par=True)

    for k in range(4):
        phase1_a(4 + k, x_own, xs_b, xn_b, Bxs_b, Bxn_b)
        phase1_b(3 + k, xn_b, Bxn_b)
        u_proj(2 * k, 0)
        u_proj(2 * k + 1, 0)
    phase1_b(7, xn_b, Bxn_b)
    for g in range(8):
        u_proj(g, 1)
    P.op("sp", lambda e: e.dma_start(out=wraw, in_=w_sgu.rearrange("g t s -> t g s")), dma_into=Bwraw)
    P.op("dve", lambda e: e.tensor_copy(out=wrawb, in_=wraw), reads=[Bwraw], writes=[Bwrawb])
    for gh in range(2):
        bi = bank()
        transposes(bi, [wrawb[:, gh * 4 + i, :] for i in range(4)], [Bwrawb])
        for i in range(4):
            P.op("dve", lambda e, bi=bi, gh=gh, i=i: e.tensor_tensor(out=wsT[:, gh * 4 + i, :], in0=ps[bi][:, i * 128:(i + 1) * 128],
                                                                    in1=mask01, op=ALU.mult),
                 reads=[Bps[bi], Bsg], writes=[BwsT], par=True)
    Wv = [wtile_cols(w_in, OV + half * 512, 512) for half in range(2)]
    def sgu_v(t):
        vi = t % 2
        bis = []
        for half in range(2):
            bi = bank()
            bis.append(bi)
            mm_group(bi, [(ps[bi][:, :], aT[:, c, t * 128:(t + 1) * 128], Wv[half][0][:, c, :]) for c in range(16)],
                     [BaT[t], Wv[half][1]])
        for half in range(2):
            bi = bis[half]
            P.op("act", lambda e, bi=bi, vi=vi, half=half: e.activation(out=vg[vi][:, half * 512:(half + 1) * 512], in_=ps[bi][:, :],
                                                                        func=AF.Gelu_apprx_tanh),
                 reads=[Bps[bi]], writes=[Bvg[vi]], par=(half > 0))

    def sgu_chain(t):
        vi = t % 2
        rstd, brs = rms_rstd(vg[vi], [Bvg[vi]], 1024, vn[0], Bvn[0])
        P.op("dve", lambda e, vi=vi, rstd=rstd: e.scalar_tensor_tensor(out=vn[0], in0=vg[vi], scalar=rstd, in1=gsgu_bc,
                                                                      op0=ALU.mult, op1=ALU.mult),
             reads=[Bvg[vi], brs, Bsg], writes=[Bvn[0]])

    def sgu_mix(t):
        for gh in range(2):
            bi = bank()
            grp = []
            for gl in range(4):
                g = gh * 4 + gl
                P.op("pe", lambda e, bi=bi, gl=gl, g=g: e.matmul(ps[bi][:, gl * 128:(gl + 1) * 128], vn[0][:, g * 128:(g + 1) * 128],
                                                                  wsT[:, g, :], start=True, stop=True),
                     reads=[Bvn[0], BwsT], writes=[Bps[bi]], group=grp)
            P.op("dve", lambda e, bi=bi, gh=gh: e.tensor_tensor(out=tmpm[0], in0=ps[bi][:, :], in1=bsgu_bc[:, gh * 512:(gh + 1) * 512],
                                                               op=ALU.add),
                 reads=[Bps[bi], Bsg], writes=[Btmpm[0]])
            o_ap = sguT[:, gh * 4:gh * 4 + 4, t * 128:(t + 1) * 128]
            P.op("dve", lambda e, o_ap=o_ap: e.tensor_tensor(out=o_ap, in0=o_ap, in1=tmpm[0][:, :].rearrange("p (g n) -> p g n", g=4),
                                                             op=ALU.mult),
                 reads=[Btmpm[0]] + BsguT[gh * 4:gh * 4 + 4], writes=BsguT[gh * 4:gh * 4 + 4], par=True)

    for t in range(8):
        sgu_v(t)
        if t >= 1:
            sgu_mix(t - 1)
        sgu_chain(t)
    sgu_mix(7)
    dump("sguT", pd(96, 8 * 1024), BsguT, [128, 8 * 1024])

    mergedT = pd(64, 16 * 1024).rearrange("p (c n) -> p c n", c=16)
    BmT = [Buf("mT%d" % i) for i in range(16)]
    sgt = [pd(112, 512, F32), pd(114, 512, F32)]
    Bsgt = [Buf("sgt0"), Buf("sgt1")]
    m1t = [pd(116, 512, F32), pd(118, 512, F32)]
    Bm1t = [Buf("m1t0"), Buf("m1t1")]
    sctr = [0]
    for pas in range(2):
        for cg in range(4):
            Wg, BWg_ = wtile_cols(w_in, (OG0 if pas == 0 else OG1) + cg * 512, 512)
            if pas == 0:
                Wo, BWo = wtile_cols(w_o_attn, cg * 512, 512)
                KO, srcT, Bsrc = 16, attnT, BattnT
            else:
                Wo, BWo = wtile_cols(w_o_sgu, cg * 512, 512, kc=8)
                KO, srcT, Bsrc = 8, sguT, BsguT
            for c4 in range(4):
                c = cg * 4 + c4
                for tg in range(2):
                    tsl = slice(tg * 512, (tg + 1) * 512)
                    bgt = bank()
                    mm_group(bgt, [(ps[bgt][:, :], Wg[:, k, c4 * 128:(c4 + 1) * 128], aT[:, k, tsl]) for k in range(16)],
                             [BWg_] + BaT[4 * tg:4 * tg + 4])
                    by = bank()
                    mm_group(by, [(ps[by][:, :], Wo[:, k, c4 * 128:(c4 + 1) * 128], srcT[:, k, tsl]) for k in range(KO)],
                             [BWo] + Bsrc)
                    si = sctr[0] % 2
                    sctr[0] += 1
                    bcol = pas * 16 + c
                    P.op("act", lambda e, bgt=bgt, si=si, bcol=bcol: e.activation(out=sgt[si], in_=ps[bgt][:, :], func=AF.Sigmoid,
                                                                                  bias=bg[:, bcol:bcol + 1], scale=1.0),
                         reads=[Bps[bgt], Bbg], writes=[Bsgt[si]])
                    if pas == 0:
                        P.op("dve", lambda e, by=by, si=si, c=c, tsl=tsl: e.tensor_tensor(out=mergedT[:, c, tsl], in0=ps[by][:, :], in1=sgt[si],
                                                                                         op=ALU.mult),
                             reads=[Bps[by], Bsgt[si]], writes=[BmT[c]], par=True)
                    else:
                        P.op("dve", lambda e, by=by, si=si: e.tensor_tensor(out=m1t[si], in0=ps[by][:, :], in1=sgt[si], op=ALU.mult),
                             reads=[Bps[by], Bsgt[si]], writes=[Bm1t[si]])
                        P.op("dve", lambda e, si=si, c=c, tsl=tsl: e.tensor_tensor(out=mergedT[:, c, tsl], in0=mergedT[:, c, tsl], in1=m1t[si],
                                                                                  op=ALU.add),
                             reads=[Bm1t[si], BmT[c]], writes=[BmT[c]], par=True)
    dump("mergedT", pd(64, 16 * 1024), BmT, [128, 16 * 1024])
    P.barrier()

    hreg = pd(0, 8 * 2048, F32).rearrange("p (t n) -> p t n", t=8)
    Bh = [Buf("h%d" % i) for i in range(8)]
    for t in range(8):
        P.op("sp", lambda e, t=t: e.dma_start(out=hreg[:, t, :], in_=x_own[t * 128:(t + 1) * 128, :]), dma_into=Bh[t])
    P.op("sp", lambda e: e.dma_start(out=g_bc, in_=norm_ffn_g.partition_broadcast(128)), dma_into=Bgbc)
    fT = pd(96, 16 * 1024).rearrange("p (c n) -> p c n", c=16)
    BfT = [Buf("fT%d" % i) for i in range(8)]
    xn_c = pd(128, 2048)
    Bxn_c = Buf("xnc")
    h_stats = {}

    def h_stt(t):
        rstd, brs = h_stats[t]
        P.op("dve", lambda e, t=t, rstd=rstd: e.scalar_tensor_tensor(out=xn_c, in0=hreg[:, t, :], scalar=rstd, in1=g_bc,
                                                                    op0=ALU.mult, op1=ALU.mult),
             reads=[Bh[t], brs, Bgbc], writes=[Bxn_c])
    for cg in range(4):
        Wo_, BWo_ = wtile_cols(w_out, cg * 512, 512)
        for t in range(8):
            bi = bank()
            mm_group(bi, [(ps[bi][:, :], mergedT[:, k, t * 128:(t + 1) * 128], Wo_[:, k, :]) for k in range(16)], [BWo_] + BmT)
            P.op("dve", lambda e, bi=bi, t=t, cg=cg: e.tensor_tensor(out=hreg[:, t, cg * 512:(cg + 1) * 512],
                                                                    in0=ps[bi][:, :], in1=hreg[:, t, cg * 512:(cg + 1) * 512], op=ALU.add),
                 reads=[Bps[bi], Bh[t]], writes=[Bh[t]], par=True)
            if cg == 3:
                if t < 7:
                    h_stats[t] = rms_rstd(hreg[:, t, :].rearrange("p (c n) -> p c n", c=16), [Bh[t]], D,
                                          fT[:, :, 7 * 128:8 * 128], BfT[7], par_junk=True)
                else:
                    h_stats[t] = rms_rstd(hreg[:, t, :].rearrange("p (c n) -> p c n", c=16), [Bh[t]], D,
                                          fT[:, :, 6 * 128:7 * 128], BfT[6], par_junk=True)
                if t >= 2:
                    phase1_b(t - 2, [xn_c], [Bxn_c], dst=fT, Bdst=BfT)
                if t >= 1:
                    h_stt(t - 1)
    phase1_b(6, [xn_c], [Bxn_c], dst=fT, Bdst=BfT)
    h_stt(7)
    phase1_b(7, [xn_c], [Bxn_c], dst=fT, Bdst=BfT)
    dump("h1", pd(0, 8 * 2048, F32), Bh, [128, 8 * 2048])

    actT = pd(64, 12 * 1024).rearrange("p (c n) -> p c n", c=12)
    Bact = [Buf("act%d" % i) for i in range(12)]
    sg = [pd(88, 512, F32), pd(90, 512, F32), pd(92, 512, F32), pd(94, 512, F32)]
    Bsgl = [Buf("sg%d" % i) for i in range(4)]
    gctr = [0]
    parts = [(0, 3), (3, 3), (6, 3), (9, 2)]
    outs = []
    w_down_v = w_down.rearrange("(c p) n -> p c n", p=128)
    for (cg0, ncg) in parts:
        nch = ncg * 4
        for cgi in range(ncg):
            cg = cg0 + cgi
            Wgt, BWgt = wtile_cols(w_gate, cg * 512, 512)
            Wup, BWup = wtile_cols(w_up, cg * 512, 512)
            for c4 in range(4):
                ci = cgi * 4 + c4
                for tg in range(2):
                    tsl = slice(tg * 512, (tg + 1) * 512)
                    bgt = bank()
                    mm_group(bgt, [(ps[bgt][:, :], Wgt[:, k, c4 * 128:(c4 + 1) * 128], fT[:, k, tsl]) for k in range(16)],
                             [BWgt] + BfT[4 * tg:4 * tg + 4])
                    bu = bank()
                    mm_group(bu, [(ps[bu][:, :], Wup[:, k, c4 * 128:(c4 + 1) * 128], fT[:, k, tsl]) for k in range(16)],
                             [BWup] + BfT[4 * tg:4 * tg + 4])
                    gi = gctr[0] % 4
                    gctr[0] += 1
                    P.op("act", lambda e, bgt=bgt, gi=gi: e.activation(out=sg[gi], in_=ps[bgt][:, :], func=AF.Silu),
                         reads=[Bps[bgt]], writes=[Bsgl[gi]])
                    P.op("dve", lambda e, bu=bu, gi=gi, ci=ci, tsl=tsl: e.tensor_tensor(out=actT[:, ci, tsl], in0=ps[bu][:, :], in1=sg[gi],
                                                                                       op=ALU.mult),
                         reads=[Bps[bu], Bsgl[gi]], writes=[Bact[ci]], par=True)
        last_part = (cg0 + ncg == 11)
        if last_part:
            P.op("sp", lambda e: e.dma_start(out=g_bc, in_=norm_final_g.partition_broadcast(128)), dma_into=Bgbc)
        Wds = []
        for dcg in range(4):
            src = w_down_v[:, cg0 * 4:cg0 * 4 + nch, dcg * 512:(dcg + 1) * 512]
            sld, BWd = wload([(lambda a, nch=nch: a[:, 0:nch * 512].rearrange("p (c n) -> p c n", c=nch), src)])
            Wds.append((sld[:, 0:nch * 512].rearrange("p (c n) -> p c n", c=nch), BWd))
            if not last_part:
                Wd = Wds[dcg][0]
                for t in range(8):
                    bi = bank()
                    mm_group(bi, [(ps[bi][:, :], actT[:, k, t * 128:(t + 1) * 128], Wd[:, k, :]) for k in range(nch)],
                             [BWd] + Bact[0:nch])
                    P.op("dve", lambda e, bi=bi, t=t, dcg=dcg: e.tensor_tensor(out=hreg[:, t, dcg * 512:(dcg + 1) * 512],
                                                                              in0=ps[bi][:, :], in1=hreg[:, t, dcg * 512:(dcg + 1) * 512], op=ALU.add),
                         reads=[Bps[bi], Bh[t]], writes=[Bh[t]], par=True)
        if last_part:
            junk_d = pd(96, 2048)
            Bjunk_d = Buf("junkd")
            for t in range(8):
                for dcg in range(4):
                    Wd, BWd = Wds[dcg]
                    bi = bank()
                    mm_group(bi, [(ps[bi][:, :], actT[:, k, t * 128:(t + 1) * 128], Wd[:, k, :]) for k in range(nch)],
                             [BWd] + Bact[0:nch])
                    P.op("dve", lambda e, bi=bi, t=t, dcg=dcg: e.tensor_tensor(out=hreg[:, t, dcg * 512:(dcg + 1) * 512],
                                                                              in0=ps[bi][:, :], in1=hreg[:, t, dcg * 512:(dcg + 1) * 512], op=ALU.add),
                         reads=[Bps[bi], Bh[t]], writes=[Bh[t]], par=True)
                def fin_norm(t):
                    rstd, brs = rms_rstd(hreg[:, t, :], [Bh[t]], D, junk_d, Bjunk_d)
                    P.op("dve", lambda e, t=t, rstd=rstd: e.scalar_tensor_tensor(out=hreg[:, t, :], in0=hreg[:, t, :], scalar=rstd, in1=g_bc,
                                                                                op0=ALU.mult, op1=ALU.mult),
                         reads=[Bh[t], brs, Bgbc], writes=[Bh[t]])
                    outs.append(P.op("sp", lambda e, t=t: e.dma_start(out=out_d[t * 128:(t + 1) * 128, :], in_=hreg[:, t, :]),
                                     reads=[Bh[t]], dma_out=Buf("out%d" % t)))
                if t >= 1:
                    fin_norm(t - 1)
                if t == 7:
                    fin_norm(7)
    P.run_block(final_waits=outs + list(dbg_d.values()))
    for c in reversed(ps_ctx):
        c.__exit__(None, None, None)
    arena_ctx.__exit__(None, None, None)
    return nc


def _consts(r):
    k = np.arange(128)
    ident = np.eye(128, dtype=np.float32)
    tri = np.where(k[None, :] >= k[:, None], 0.0, NEG).astype(np.float32)
    omask = np.full((128, 128), 0.0 if r == 1 else NEG, np.float32)
    mask01 = (k[None, :] >= k[:, None]).astype(np.float32)
    inv_freq = (np.float32(10000.0) ** (-(np.arange(0, 64, 2, dtype=np.float32)) / np.float32(64))).astype(np.float32)
    invf = np.tile(inv_freq, 4).reshape(128, 1).astype(np.float32)
    sgn = np.where((k % 64) < 32, -1.0, 1.0).astype(np.float32).reshape(128, 1)
    return dict(ident=ident, tri=tri, omask=omask, mask01=mask01, invf=invf, sgn=sgn)


_NC_CACHE = {}


def kernel(x, positions, norm_mix_g, w_in, b_gate, q_norm_g, w_uq, kv_norm_g, w_ukv, w_o_attn,
           sgu_norm_g, w_sgu, b_sgu, w_o_sgu, w_out, norm_ffn_g, w_gate_ffn, w_up_ffn,
           w_down_ffn, norm_final_g):
    f = lambda a: np.ascontiguousarray(np.asarray(a, dtype=np.float32))
    x = f(x)
    positions = np.ascontiguousarray(np.asarray(positions, dtype=np.int32))
    shared = dict(
        norm_mix_g=f(norm_mix_g).reshape(D), w_in=f(w_in).reshape(D, 7232), b_gate=f(b_gate).reshape(4096),
        q_norm_g=f(q_norm_g).reshape(512), w_uq=f(w_uq).reshape(512, 3072), kv_norm_g=f(kv_norm_g).reshape(512),
        w_ukv=f(w_ukv).reshape(512, 4096), w_o_attn=f(w_o_attn).reshape(D, D), sgu_norm_g=f(sgu_norm_g).reshape(1024),
        w_sgu=f(w_sgu).reshape(8, 128, 128), b_sgu=f(b_sgu).reshape(1024), w_o_sgu=f(w_o_sgu).reshape(1024, D),
        w_out=f(w_out).reshape(D, D), norm_ffn_g=f(norm_ffn_g).reshape(D), w_gate_ffn=f(w_gate_ffn).reshape(D, DFF),
        w_up_ffn=f(w_up_ffn).reshape(D, DFF), w_down_ffn=f(w_down_ffn).reshape(DFF, D), norm_final_g=f(norm_final_g).reshape(D),
    )
    in_maps = []
    for c in range(8):
        b, r = c // 2, c % 2
        xb = x[b].reshape(16, 128, D)
        pb = positions[b].reshape(16, 128)
        m = dict(shared)
        m["x_own"] = np.ascontiguousarray(xb[r::2].reshape(T, D))
        m["x_oth"] = np.ascontiguousarray(xb[1 - r::2].reshape(T, D))
        m["pos"] = np.ascontiguousarray(np.concatenate([pb[r::2].reshape(-1), pb[1 - r::2].reshape(-1)])[None, :])
        m.update(_consts(r))
        in_maps.append(m)
    if "nc" not in _NC_CACHE:
        _NC_CACHE["nc"] = build_program()
    nc = _NC_CACHE["nc"]
    res = run_bass_kernel_spmd(nc, in_maps, core_ids=list(range(8)))
    out = np.empty((4, 16, 128, D), np.float32)
    for c in range(8):
        b, r = c // 2, c % 2
        out[b, r::2] = np.asarray(res.results[c]["out"], dtype=np.float32).reshape(8, 128, D)
    kernel.last_results = res
    return out.reshape(4, S, D)
```
